# Optimizing a Trainium2 kernel written in Bass

```python
import math
import jax, jax.numpy as jnp
from jax import lax
import numpy as np

D_MODEL = 1024
BATCH = 32
SEQ = 256
DEPTH = 2
DEC_BATCH = 4
DEC_SEQ = 2048
PAST_LEN = 512

GRID_W = 64
HEAD_DIM = 64
MIX_W = D_MODEL
SGU_W = MIX_W // 4
SGU_GROUPS = SGU_W // HEAD_DIM
SGU_CHUNK = 128
DIFF_HEADS = (MIX_W // 4) // HEAD_DIM
DIFF_DV = HEAD_DIM
DIFF_DQK = HEAD_DIM // 2
DIFF_SCALE = DIFF_DQK ** -0.5
GQA_HEADS = (MIX_W // 2) // HEAD_DIM
GQA_KV_HEADS = GQA_HEADS // 4
GQA_GROUP = GQA_HEADS // GQA_KV_HEADS
GQA_SCALE = HEAD_DIM ** -0.5
IN_SIZES = (SGU_W, SGU_W, DIFF_HEADS * 2 * DIFF_DQK, DIFF_HEADS * 2 * DIFF_DQK, DIFF_HEADS * DIFF_DV, GQA_HEADS * HEAD_DIM, GQA_KV_HEADS * HEAD_DIM, GQA_KV_HEADS * HEAD_DIM)
IN_W = 2 * SGU_W + 2 * DIFF_HEADS * 2 * DIFF_DQK + DIFF_HEADS * DIFF_DV + (GQA_HEADS + 2 * GQA_KV_HEADS) * HEAD_DIM
QBLOCK = 128
ADA_CHUNKS = 6
N_KEYS = 128
N_EXPERTS = N_KEYS * N_KEYS
PEER_HEADS = 8
PEER_TOPK = 16
PEER_QDIM = 128
PEER_HALF = PEER_QDIM // 2
TOKEN_BLOCK = 128
ROPE_THETA = 10000.0
EPS = 1e-6

kernel_name = 'hybrid_diffusion_prefix_step'


def rms_norm(x, g):
    xf = x.astype(jnp.float32)
    y = xf * lax.rsqrt(jnp.mean(xf * xf, axis=-1, keepdims=True) + EPS)
    return (y * g.astype(jnp.float32)).astype(x.dtype)


def axial_rope(n_tokens, dim):
    rows = n_tokens // GRID_W
    row = jnp.repeat(jnp.arange(rows, dtype=jnp.float32), GRID_W)
    col = jnp.tile(jnp.arange(GRID_W, dtype=jnp.float32), rows)
    quarter = dim // 4
    inv = ROPE_THETA ** (-jnp.arange(quarter, dtype=jnp.float32) / quarter)
    ang = jnp.concatenate([row[:, None] * inv, col[:, None] * inv], axis=-1)
    return jnp.cos(ang), jnp.sin(ang)


def apply_rope(x, cos, sin):
    shape = (cos.shape[0],) + (1,) * (x.ndim - 3) + (cos.shape[1],)
    c = cos.reshape(shape).astype(x.dtype)
    s = sin.reshape(shape).astype(x.dtype)
    x1, x2 = jnp.split(x, 2, axis=-1)
    return jnp.concatenate([x1 * c - x2 * s, x2 * c + x1 * s], axis=-1)


def sweep_query_blocks(fn, q):
    B, L = q.shape[0], q.shape[1]
    nb = L // QBLOCK
    qb = jnp.moveaxis(q.reshape((B, nb, QBLOCK) + q.shape[2:]), 1, 0)
    out = jnp.moveaxis(lax.map(fn, qb), 0, 1)
    return out.reshape((B, L) + out.shape[3:])


def peer_ffn(h, wq, subkeys, u_tab, v_tab):
    B, L, D = h.shape
    xt = h.reshape(B * L // TOKEN_BLOCK, TOKEN_BLOCK, D)

    def block(xb):
        T = xb.shape[0]
        q = (xb @ wq).reshape(T, PEER_HEADS, 2, PEER_HALF)
        s = jnp.einsum('thxk,xnk->thxn', q, subkeys).astype(jnp.float32)
        sv, si = lax.top_k(s, PEER_TOPK)
        cand = (sv[:, :, 0, :, None] + sv[:, :, 1, None, :]).reshape(T, PEER_HEADS, PEER_TOPK * PEER_TOPK)
        cidx = (si[:, :, 0, :, None] * N_KEYS + si[:, :, 1, None, :]).reshape(T, PEER_HEADS, PEER_TOPK * PEER_TOPK)
        top_s, top_i = lax.top_k(cand, PEER_TOPK)
        expert = jnp.take_along_axis(cidx, top_i, axis=-1)
        gate = jax.nn.softmax(top_s, axis=-1)
        act = jax.nn.gelu(jnp.einsum('thkd,td->thk', u_tab[expert], xb))
        return jnp.einsum('thk,thkd->td', (gate * act).astype(xb.dtype), v_tab[expert])

    return lax.map(block, xt).reshape(B, L, D)


def token_mixers(h, ctx, rope, lam_init, w_in, sgu_norm_g, sgu_w, sgu_b, lq1, lk1, lq2, lk2, subln_g, qn_g, kn_g, w_out):
    B, L, _ = h.shape
    splits = np.cumsum(IN_SIZES)[:-1].tolist()
    a_u, a_v, d_q, d_k, d_v, g_q, g_k, g_v = jnp.split(h @ w_in, splits, axis=-1)

    a_u = jax.nn.gelu(a_u)
    a_v = rms_norm(jax.nn.gelu(a_v), sgu_norm_g).reshape(B, L // SGU_CHUNK, SGU_CHUNK, SGU_GROUPS, HEAD_DIM)
    a_mix = jnp.einsum('gpq,bnqgc->bnpgc', sgu_w, a_v) + sgu_b.T[:, :, None]
    out_a = a_u * a_mix.reshape(B, L, SGU_W)

    d_q = d_q.reshape(B, L, DIFF_HEADS, 2, DIFF_DQK)
    d_k = d_k.reshape(B, L, DIFF_HEADS, 2, DIFF_DQK)
    d_v = d_v.reshape(B, L, DIFF_HEADS, DIFF_DV)
    g_q = rms_norm(g_q.reshape(B, L, GQA_HEADS, HEAD_DIM), qn_g)
    g_k = rms_norm(g_k.reshape(B, L, GQA_KV_HEADS, HEAD_DIM), kn_g)
    g_v = g_v.reshape(B, L, GQA_KV_HEADS, HEAD_DIM)
    ctx_out = (d_k, d_v, g_k, g_v)
    if ctx is None:
        dk_all, dv_all, gk_all, gv_all = ctx_out
    else:
        (cos_d, sin_d), (cos_g, sin_g) = rope
        c_dk, c_dv, c_gk, c_gv = ctx
        d_q = apply_rope(d_q, cos_d, sin_d)
        g_q = apply_rope(g_q, cos_g, sin_g)
        dk_all = jnp.concatenate([apply_rope(d_k, cos_d, sin_d), c_dk.astype(d_k.dtype)], axis=1)
        dv_all = jnp.concatenate([d_v, c_dv.astype(d_v.dtype)], axis=1)
        gk_all = jnp.concatenate([apply_rope(g_k, cos_g, sin_g), c_gk.astype(g_k.dtype)], axis=1)
        gv_all = jnp.concatenate([g_v, c_gv.astype(g_v.dtype)], axis=1)

    f32 = jnp.float32
    lam = (jnp.exp(jnp.sum(lq1.astype(f32) * lk1.astype(f32)))
           - jnp.exp(jnp.sum(lq2.astype(f32) * lk2.astype(f32))) + lam_init)

    def diff_block(qb):
        s = jnp.einsum('bqhxd,bkhxd->bhxqk', qb, dk_all).astype(f32) * DIFF_SCALE
        p = jax.nn.softmax(s, axis=-1)
        w = p[:, :, 0] - lam * p[:, :, 1]
        return jnp.einsum('bhqk,bkhd->bqhd', w.astype(dv_all.dtype), dv_all)

    out_b = sweep_query_blocks(diff_block, d_q)
    out_b = (rms_norm(out_b, subln_g) * (1.0 - lam_init)).reshape(B, L, DIFF_HEADS * DIFF_DV)

    g_q = g_q.reshape(B, L, GQA_KV_HEADS, GQA_GROUP, HEAD_DIM)

    def gqa_block(qb):
        s = jnp.einsum('bqhgd,bkhd->bhgqk', qb, gk_all).astype(f32) * GQA_SCALE
        p = jax.nn.softmax(s, axis=-1)
        return jnp.einsum('bhgqk,bkhd->bqhgd', p.astype(gv_all.dtype), gv_all)

    out_c = sweep_query_blocks(gqa_block, g_q).reshape(B, L, GQA_HEADS * HEAD_DIM)

    out = jnp.concatenate([out_a, out_b, out_c], axis=-1) @ w_out
    return out, ctx_out


def run_trunk(x, cvec, ctx_cache, rope, p):
    new_ctx = []
    for l in range(DEPTH):
        lam_init = 0.8 - 0.6 * math.exp(-0.3 * l)
        mod = jax.nn.silu(cvec) @ p['ada_w'][l] + p['ada_b'][l]
        sh1, sc1, g1, sh2, sc2, g2 = jnp.split(mod[:, None, :], ADA_CHUNKS, axis=-1)
        h = rms_norm(x, p['norm1_g'][l]) * (1.0 + sc1) + sh1
        ctx_l = None if ctx_cache is None else tuple(t[:, l] for t in ctx_cache)
        mix, kv = token_mixers(h, ctx_l, rope, lam_init, p['w_in'][l], p['sgu_norm_g'][l], p['sgu_w'][l], p['sgu_b'][l],
                               p['diff_lq1'][l], p['diff_lk1'][l], p['diff_lq2'][l], p['diff_lk2'][l], p['diff_subln_g'][l],
                               p['gqa_qnorm_g'][l], p['gqa_knorm_g'][l], p['w_out'][l])
        x = x + g1 * mix
        h = rms_norm(x, p['norm2_g'][l]) * (1.0 + sc2) + sh2
        x = x + g2 * peer_ffn(h, p['peer_wq'][l], p['peer_subkeys'][l], p['peer_u'][l], p['peer_v'][l])
        if ctx_cache is None:
            new_ctx.append(kv)
    return rms_norm(x, p['final_g']), new_ctx


def setup_inputs(seed: int = 0) -> dict:
    key = jax.random.key(seed)
    ks = list(jax.random.split(key, 32))

    def nrm(i, shape, scale=1.0):
        return scale * jax.random.normal(ks[i], shape, jnp.float32)

    def gain(i, shape):
        return 1.0 + 0.01 * jax.random.normal(ks[i], shape, jnp.float32)

    D = D_MODEL
    return {
        'x_prompt': nrm(0, (BATCH, SEQ, D)),
        'x_sample': nrm(1, (DEC_BATCH, DEC_SEQ, D)),
        'cache_diff_k': nrm(2, (DEC_BATCH, DEPTH, PAST_LEN, DIFF_HEADS, 2, DIFF_DQK)),
        'cache_diff_v': nrm(3, (DEC_BATCH, DEPTH, PAST_LEN, DIFF_HEADS, DIFF_DV)),
        'cache_gqa_k': nrm(4, (DEC_BATCH, DEPTH, PAST_LEN, GQA_KV_HEADS, HEAD_DIM)),
        'cache_gqa_v': nrm(5, (DEC_BATCH, DEPTH, PAST_LEN, GQA_KV_HEADS, HEAD_DIM)),
        'c': nrm(6, (DEC_BATCH, D)),
        'c_ctx': nrm(7, (D,)),
        'ada_w': nrm(8, (DEPTH, D, ADA_CHUNKS * D), 0.5 * D ** -0.5),
        'ada_b': nrm(9, (DEPTH, ADA_CHUNKS * D), 0.01),
        'norm1_g': gain(10, (DEPTH, D)),
        'norm2_g': gain(11, (DEPTH, D)),
        'w_in': nrm(12, (DEPTH, D, IN_W), D ** -0.5),
        'sgu_norm_g': gain(13, (DEPTH, SGU_W)),
        'sgu_w': nrm(14, (DEPTH, SGU_GROUPS, SGU_CHUNK, SGU_CHUNK), SGU_CHUNK ** -0.5),
        'sgu_b': gain(15, (DEPTH, SGU_GROUPS, SGU_CHUNK)),
        'diff_lq1': nrm(16, (DEPTH, DIFF_DQK), 0.1),
        'diff_lk1': nrm(17, (DEPTH, DIFF_DQK), 0.1),
        'diff_lq2': nrm(18, (DEPTH, DIFF_DQK), 0.1),
        'diff_lk2': nrm(19, (DEPTH, DIFF_DQK), 0.1),
        'diff_subln_g': gain(20, (DEPTH, DIFF_DV)),
        'gqa_qnorm_g': gain(21, (DEPTH, HEAD_DIM)),
        'gqa_knorm_g': gain(22, (DEPTH, HEAD_DIM)),
        'w_out': nrm(23, (DEPTH, MIX_W, D), MIX_W ** -0.5),
        'peer_wq': nrm(24, (DEPTH, D, PEER_HEADS * PEER_QDIM), D ** -0.5),
        'peer_subkeys': nrm(25, (DEPTH, 2, N_KEYS, PEER_HALF), PEER_HALF ** -0.5),
        'peer_u': nrm(26, (DEPTH, N_EXPERTS, D), D ** -0.5),
        'peer_v': nrm(27, (DEPTH, N_EXPERTS, D), 0.1),
        'final_g': gain(28, (D,)),
    }


def reference(x_prompt, x_sample, cache_diff_k, cache_diff_v, cache_gqa_k, cache_gqa_v, c, c_ctx,
              ada_w, ada_b, norm1_g, norm2_g, w_in, sgu_norm_g, sgu_w, sgu_b,
              diff_lq1, diff_lk1, diff_lq2, diff_lk2, diff_subln_g, gqa_qnorm_g, gqa_knorm_g, w_out,
              peer_wq, peer_subkeys, peer_u, peer_v, final_g):
    p = {'ada_w': ada_w, 'ada_b': ada_b, 'norm1_g': norm1_g, 'norm2_g': norm2_g, 'w_in': w_in,
         'sgu_norm_g': sgu_norm_g, 'sgu_w': sgu_w, 'sgu_b': sgu_b,
         'diff_lq1': diff_lq1, 'diff_lk1': diff_lk1, 'diff_lq2': diff_lq2, 'diff_lk2': diff_lk2,
         'diff_subln_g': diff_subln_g, 'gqa_qnorm_g': gqa_qnorm_g, 'gqa_knorm_g': gqa_knorm_g, 'w_out': w_out,
         'peer_wq': peer_wq, 'peer_subkeys': peer_subkeys, 'peer_u': peer_u, 'peer_v': peer_v, 'final_g': final_g}

    c_prompt = jnp.broadcast_to(c_ctx, (x_prompt.shape[0], c_ctx.shape[0]))
    y_prompt, ctx_layers = run_trunk(x_prompt, c_prompt, None, None, p)
    new_diff_k = jnp.stack([kv[0] for kv in ctx_layers], axis=1)
    new_diff_v = jnp.stack([kv[1] for kv in ctx_layers], axis=1)
    new_gqa_k = jnp.stack([kv[2] for kv in ctx_layers], axis=1)
    new_gqa_v = jnp.stack([kv[3] for kv in ctx_layers], axis=1)

    n_lat = x_sample.shape[1]
    rope = (axial_rope(n_lat, DIFF_DQK), axial_rope(n_lat, HEAD_DIM))
    y_sample, _ = run_trunk(x_sample, c, (cache_diff_k, cache_diff_v, cache_gqa_k, cache_gqa_v), rope, p)

    return (y_prompt, y_sample, new_diff_k, new_diff_v, new_gqa_k, new_gqa_v)
```

```python
import math
import os
from contextlib import ExitStack

import numpy as np
import concourse.bass as bass
import concourse.mybir as mybir
from concourse.bass_utils import run_bass_kernel_spmd

F32 = mybir.dt.float32
BF16 = mybir.dt.bfloat16
U32 = mybir.dt.uint32
AF = mybir.ActivationFunctionType
ALU = mybir.AluOpType
AX = mybir.AxisListType

D = 1024
L = 2
NTOK = 2048
NB = 16
PAST = 512
NKB = 20
EPS = 1e-6
DIFF_SCALE = 32 ** -0.5
GQA_SCALE = 64 ** -0.5
EPOCH = 12000
TB = 256
NEG = -30000.0


class Sched:
    def __init__(self, nc, n_dma_sems=48):
        self.nc = nc
        self.eng = {"pe": nc.tensor, "act": nc.scalar, "dve": nc.vector,
                    "pool": nc.gpsimd, "sp": nc.sync}
        self.cnt = {e: 0 for e in self.eng}
        self.sems = {e: [] for e in self.eng}
        self.waited = {e: {} for e in self.eng}
        self.n_hw = n_dma_sems
        self.dma_sems = [nc.alloc_semaphore(name=f"dq{i}") for i in range(n_dma_sems)]
        self.dma_val = [0] * n_dma_sems
        self.dma_free = list(range(n_dma_sems))
        self.dma_out = []
        self.dma_waited = {e: {} for e in self.eng}
        self.last_w = {}
        self.readers = {}

    def _sem_for(self, e, n):
        ep = (n - 1) // EPOCH
        while len(self.sems[e]) <= ep:
            self.sems[e].append(self.nc.alloc_semaphore(name=f"s_{e}_{len(self.sems[e])}"))
        return self.sems[e][ep], (n - 1) % EPOCH + 1

    def _wait(self, e, tok):
        eng = self.eng[e]
        if tok[0] == "dma":
            _, idx, val = tok
            if self.dma_waited[e].get(idx, 0) >= val:
                return
            eng.wait_ge(self.dma_sems[idx], val)
            self.dma_waited[e][idx] = val
        else:
            src, n = tok
            if src == e and e == "pe":
                return
            if self.waited[e].get(src, 0) >= n:
                return
            if src == e:
                n = self.cnt[e]
            sem, v = self._sem_for(src, n)
            eng.wait_ge(sem, v)
            self.waited[e][src] = n

    def _deps(self, reads, writes, e=None):
        deps = []
        for k in reads:
            if k in self.last_w:
                deps.append((k, self.last_w[k]))
        for k in writes:
            if k in self.last_w:
                deps.append((k, self.last_w[k]))
            deps.extend((k, t) for t in self.readers.get(k, ()))
        out = []
        for k, t in deps:
            if isinstance(k, tuple) and k and k[0] == "ps" and t[0] == e:
                continue
            out.append(t)
        return out

    def _record(self, tok, reads, writes):
        for k in reads:
            self.readers.setdefault(k, []).append(tok)
        for k in writes:
            self.last_w[k] = tok
            self.readers[k] = []

    def op(self, e, fn, reads=(), writes=(), inc=True):
        for d in self._deps(reads, writes, e):
            self._wait(e, d)
        inst = fn(self.eng[e])
        if not inc:
            tok = (e, self.cnt[e] + 1)
            self._record(tok, reads, writes)
            return tok
        self.cnt[e] += 1
        n = self.cnt[e]
        sem, _ = self._sem_for(e, n)
        inst.then_inc(sem, 1)
        self._record((e, n), reads, writes)
        return (e, n)

    def dma(self, e, out, in_, reads=(), writes=(), **kw):
        for d in self._deps(reads, writes):
            self._wait(e, d)
        if e == "pool":
            idx = len(self.dma_sems)
            self.dma_sems.append(self.nc.alloc_semaphore(name=f"sw{idx}"))
            self.dma_val.append(0)
        else:
            while not self.dma_free:
                idx, val = self.dma_out.pop(0)
                self._wait(e, ("dma", idx, val))
                if idx < self.n_hw:
                    self.dma_free.append(idx)
            idx = self.dma_free.pop(0)
        self.dma_val[idx] += 16
        val = self.dma_val[idx]
        self.eng[e].dma_start(out=out, in_=in_, **kw).then_inc(self.dma_sems[idx], 16)
        self.dma_out.append((idx, val))
        tok = ("dma", idx, val)
        self._record(tok, reads, writes)
        return tok

    def bg_dma(self, e, out, in_, **kw):
        self.n_bg = getattr(self, "n_bg", 0) + 1
        sem = self.nc.alloc_semaphore(name=f"bg{self.n_bg}")
        self.eng[e].dma_start(out=out, in_=in_, **kw).then_inc(sem, 16)
        return (sem, 16)

    def wait_bg(self, e, tok):
        self.eng[e].wait_ge(tok[0], tok[1])

    def barrier(self):
        toks = [(e, self.cnt[e]) for e in self.eng if self.cnt[e] > 0]
        dmas = list(self.dma_out)
        for e in self.eng:
            for t in toks:
                if not (t[0] == e and e in ("pe", "sp")):
                    self._wait(e, t)
            for idx, val in dmas:
                self._wait(e, ("dma", idx, val))
        self.dma_out = []
        self.dma_free = list(range(self.n_hw))
        self.last_w = {}
        self.readers = {}


def build_program(n_layers=L, stop_after=None, debug=False, p2_passes=None, p2_chunks=128):
    nc = bass.Bass("TRN2", target_bir_lowering=False)

    def din(name, shape, dt=F32):
        return nc.dram_tensor(name, list(shape), dt, kind="ExternalInput").ap()

    def dout(name, shape, dt=F32):
        return nc.dram_tensor(name, list(shape), dt, kind="ExternalOutput").ap()

    x_in = din("x", [NTOK, D])
    cvec = din("cvec", [128, 8])
    cdk = din("cdk", [L, PAST, 256])
    cdv = din("cdv", [L, PAST, 256])
    cgk = din("cgk", [L, PAST, 128])
    cgv = din("cgv", [L, PAST, 128])
    maskb_d = din("maskb", [128, NKB * 8])
    cosd_d = din("cosd", [NTOK, 16])
    sind_d = din("sind", [NTOK, 16])
    cosg_d = din("cosg", [NTOK, 32])
    sing_d = din("sing", [NTOK, 32])
    ada_w = din("ada_w", [L, D, 6 * D])
    ada_b = din("ada_b", [L, 6 * D])
    norm1_g = din("norm1_g", [L, D])
    norm2_g = din("norm2_g", [L, D])
    w_in = din("w_in", [L, D, 2048])
    sgu_norm_g = din("sgu_norm_g", [L, 256])
    sgu_wT = din("sgu_wT", [L, 4, 128, 128])
    sgu_bT = din("sgu_bT", [L, 128, 4])
    lq1 = din("lq1", [L, 32]); lk1 = din("lk1", [L, 32])
    lq2 = din("lq2", [L, 32]); lk2 = din("lk2", [L, 32])
    subln_g = din("subln_g", [L, 64])
    qn_g = din("qn_g", [L, 64]); kn_g = din("kn_g", [L, 64])
    w_out = din("w_out", [L, D, D])
    wq = din("wq", [L, D, D])
    skbd = din("skbd", [L, 128, 256])
    ut = din("ut", [L, 128, 128, 8, 128])
    vt = din("vt", [L, 128 * 128, D])
    final_g = din("final_g", [1, D])

    y_out = dout("y", [NTOK, D])
    ndk = dout("ndk", [L, NTOK, 256])
    ndv = dout("ndv", [L, NTOK, 256])
    ngk = dout("ngk", [L, NTOK, 128])
    ngv = dout("ngv", [L, NTOK, 128])

    xs = nc.dram_tensor("xs", [NTOK, D], F32, kind=("ExternalOutput" if debug else "Internal")).ap()

    h2s = nc.dram_tensor("h2s", [NB, 128, 8 * 128], BF16, kind="Internal").ap()
    ijg = nc.dram_tensor("ijg", [3, 128, NTOK], F32, kind="Internal").ap()
    utb_l = [nc.dram_tensor(f"utb{i}", [128, 128, 8 * 128], BF16, kind="Internal").ap() for i in range(L)]
    vtb_l = [nc.dram_tensor(f"vtb{i}", [128, 128, D], BF16, kind="Internal").ap() for i in range(L)]

    with ExitStack() as G:
        def sbg(name, shape, dt=F32):
            return G.enter_context(nc.sbuf_tensor(name, list(shape), dt))

        ps = G.enter_context(nc.psum_tensor("ps", [128, 8, 512], F32))

        def psb(b):
            return ps[:, b, :].bitcast(BF16)

        def pb(b):
            return ("ps", b)

        S = Sched(nc)

        iota_f = sbg("iota_f", [128, 128])
        pid = sbg("pid", [128, 1])
        idb = sbg("idb", [128, 128], BF16)
        idf = sbg("idf", [128, 128])
        iota16 = sbg("iota16", [128, 16])
        iota_b = sbg("iota_b", [128, 128], BF16)
        maskb = sbg("maskb_t", [128, NKB * 8])
        mhalf = sbg("mhalf", [128, 16])
        c4u = sbg("c4u", [128, 1], U32)
        c15u = sbg("c15u", [128, 1], U32)
        g2t = sbg("g2t", [128, D])
        fgb = sbg("fgb", [128, D])
        neglam = sbg("neglam", [128, 1])
        S.op("pool", lambda e: e.iota(iota_f[:], [[1, 128]], base=0, channel_multiplier=0,
                                      allow_small_or_imprecise_dtypes=True), writes=["iota_f"])
        S.op("pool", lambda e: e.iota(pid[:], [[0, 1]], base=0, channel_multiplier=1,
                                      allow_small_or_imprecise_dtypes=True), writes=["pid"])
        S.op("pool", lambda e: e.iota(iota16[:], [[1, 16]], base=0, channel_multiplier=0,
                                      allow_small_or_imprecise_dtypes=True), writes=["iota16"])
        S.op("dve", lambda e: e.tensor_scalar(idb[:], iota_f[:], pid[:, 0:1], None, ALU.is_equal),
             reads=["iota_f", "pid"], writes=["idb"])
        S.op("dve", lambda e: e.tensor_scalar(idf[:], iota_f[:], pid[:, 0:1], None, ALU.is_equal),
             reads=["iota_f", "pid"], writes=["idf"])
        S.op("dve", lambda e: e.memset(mhalf[:], -0.5), writes=["mhalf"])
        S.op("dve", lambda e: e.tensor_copy(iota_b[:], iota_f[:]), reads=["iota_f"], writes=["iota_b"])
        S.op("pool", lambda e: e.iota(c4u[:], [[0, 1]], base=4, channel_multiplier=0), writes=["c4u"])
        S.op("pool", lambda e: e.iota(c15u[:], [[0, 1]], base=15, channel_multiplier=0), writes=["c15u"])
        S.dma("sp", maskb[:], maskb_d, writes=["maskb"])
        S.dma("sp", fgb[:], final_g.to_broadcast([128, D]), writes=["fgb"])
        S.barrier()

        def rstd_from_ssq(ssq_ap, out_ap, n, width, key_in, key_out, tmpkey, tmp_ap):
            S.op("dve", lambda e: e.tensor_scalar(tmp_ap, ssq_ap, 1.0 / width, EPS, ALU.mult, ALU.add),
                 reads=[key_in], writes=[tmpkey])
            S.op("pool", lambda e: e.tensor_tensor(out_ap, tmp_ap, mhalf[:, 0:n], ALU.pow),
                 reads=[tmpkey, "mhalf"], writes=[key_out])

        for l in range(n_layers):
            lam_init = 0.8 - 0.6 * math.exp(-0.3 * l)
            PM = ExitStack()
            G.callback(PM.close)
            mod = PM.enter_context(nc.sbuf_tensor(f"mod_{l}", [128, 6, D], F32))
            with ExitStack() as P:
                def sb(name, shape, dt=F32):
                    return P.enter_context(nc.sbuf_tensor(f"{name}_{l}", list(shape), dt))
                cv = sb("cv", [128, 8]); scv = sb("scv", [128, 8])
                crep = sb("crep", [128, 8, 128])
                awt = [sb(f"awt{i}", [128, 8, 512]) for i in range(3)]
                g1b = sb("g1b", [128, D]); g2b = sb("g2b", [128, D])
                lqk = sb("lqk", [128, 4, 32]); lss = sb("lss", [128, 2]); lex = sb("lex", [128, 2])
                ljunk = sb("ljunk", [128, 32])
                S.dma("sp", cv[:], cvec, writes=["cv"])
                S.dma("sp", mod[:].rearrange("p a b -> p (a b)"),
                      ada_b[l:l + 1, :].to_broadcast([128, 6 * D]), writes=["mod"])
                S.dma("sp", g1b[:], norm1_g[l:l + 1, :].to_broadcast([128, D]), writes=["g1b"])
                S.dma("sp", g2b[:], norm2_g[l:l + 1, :].to_broadcast([128, D]), writes=["g2b"])
                for i, t in enumerate((lq1, lk1, lq2, lk2)):
                    S.dma("sp", lqk[:, i, :], t[l:l + 1, :].to_broadcast([128, 32]), writes=[("lqk", i)])
                S.op("act", lambda e: e.activation(scv[:], cv[:], AF.Silu), reads=["cv"], writes=["scv"])
                S.op("dve", lambda e: e.tensor_copy(crep[:], scv[:].unsqueeze(2).to_broadcast([128, 8, 128])),
                     reads=["scv"], writes=["crep"])
                awv = ada_w[l].rearrange("(dc p) n -> p dc n", p=128)
                modf = mod[:].rearrange("p a b -> p (a b)")
                for c in range(12):
                    a = awt[c % 3]
                    if c == 0:
                        for c0 in range(3):
                            S.dma("sp", awt[c0][:], awv[:, :, c0 * 512:(c0 + 1) * 512], writes=[("awt", c0)])
                    bk = c % 2
                    for dc in range(8):
                        S.op("pe", lambda e: e.matmul(ps[:, bk, :], crep[:, dc, :], a[:, dc, :],
                                                      start=(dc == 0), stop=(dc == 7)),
                             reads=["crep", ("awt", c % 3)], writes=[pb(bk)])
                    if c + 3 < 12:
                        S.dma("sp", a[:], awv[:, :, (c + 3) * 512:(c + 4) * 512], writes=[("awt", c % 3)])
                    S.op("dve", lambda e: e.tensor_tensor(modf[:, c * 512:(c + 1) * 512], ps[:, bk, :],
                                                          modf[:, c * 512:(c + 1) * 512], ALU.add),
                         reads=[], writes=[pb(bk), "mod"])
                S.op("dve", lambda e: e.scalar_tensor_tensor(mod[:, 1, :], mod[:, 1, :], 1.0, g1b[:], ALU.add, ALU.mult),
                     reads=["g1b"], writes=["mod"])
                S.op("dve", lambda e: e.scalar_tensor_tensor(mod[:, 4, :], mod[:, 4, :], 1.0, g2b[:], ALU.add, ALU.mult),
                     reads=["g2b"], writes=["mod"])
                for j in range(2):
                    S.op("dve", lambda e: e.scalar_tensor_tensor(ljunk[:], lqk[:, 2 * j, :], 1.0, lqk[:, 2 * j + 1, :],
                                                                 ALU.mult, ALU.mult, accum_out=lss[:, j:j + 1]),
                         reads=[("lqk", 2 * j), ("lqk", 2 * j + 1)], writes=["ljunk", ("lss", j)])
                S.op("act", lambda e: e.activation(lex[:], lss[:], AF.Exp), reads=[("lss", 0), ("lss", 1)], writes=["lex"])
                S.op("dve", lambda e: e.scalar_tensor_tensor(neglam[:], lex[:, 1:2], -lam_init, lex[:, 0:1],
                                                             ALU.add, ALU.subtract),
                     reads=["lex"], writes=["neglam"])
                S.barrier()

            x_src = x_in if l == 0 else xs
            with ExitStack() as PA:
                def sba(name, shape, dt=F32):
                    return PA.enter_context(nc.sbuf_tensor(f"{name}_{l}", list(shape), dt))
                KTd = sba("KTd", [128, 2, NKB * 128], BF16)
                KTg = sba("KTg", [128, NKB * 128], BF16)
                QTd = sba("QTd", [128, 2, NTOK], BF16)
                QTg = sba("QTg", [128, 4, NTOK], BF16)
                Vd = sba("Vd", [128, NKB, 4, 65], BF16)
                Vg = sba("Vg", [128, NKB, 2, 65], BF16)
                mixA = sba("mixA", [128, NB, 256], BF16)
                S.op("pool", lambda e: e.memset(Vd[:].rearrange("p a b c -> p (a b c)"), 1.0), writes=["Vd"])
                S.op("pool", lambda e: e.memset(Vg[:].rearrange("p a b c -> p (a b c)"), 1.0), writes=["Vg"])
                S.barrier()

                with ExitStack() as P:
                    def sb(name, shape, dt=F32):
                        return P.enter_context(nc.sbuf_tensor(f"{name}_{l}", list(shape), dt))
                    wib = sb("wib", [128, 8, 2048], BF16)
                    swT = sb("swT", [128, 4, 128], BF16)
                    sgub = sb("sgub", [128, 4])
                    sgng = sb("sgng", [128, 256])
                    qkg = sb("qkg", [128, 10, 64])
                    rope = sb("rope", [128, NB, 96])
                    xt = [sb(f"xt{i}", [128, D]) for i in range(2)]
                    hfP = [sb(f"hf{i}", [128, D]) for i in range(2)]
                    hbP = [sb(f"hb{i}", [128, D], BF16) for i in range(2)]
                    hTP = [sb(f"hT{i}", [128, 8, 128], BF16) for i in range(2)]
                    smP = [sb(f"sm{i}", [128, 64]) for i in range(2)]
                    stP = [sb(f"st{i}", [128, 1536]) for i in range(2)]
                    auP = [sb(f"au{i}", [128, 256]) for i in range(2)]
                    avP = [sb(f"av{i}", [128, 256]) for i in range(2)]
                    avn = sb("avn", [128, 256], BF16)
                    tmpa = sb("tmpa", [128, 640])
                    gn = sb("gn", [128, 640])
                    rt = sb("rt", [128, 4, 320])
                    rq = sb("rq", [128, 4, 256])
                    qkr = sb("qkr", [128, 1152], BF16)
                    cst4 = sb("cst", [128, 4, 768], BF16)
                    wiv = w_in[l].rearrange("(dc p) n -> p dc n", p=128)
                    S.dma("pool", wib[:], wiv, writes=["wib"])
                    S.dma("pool", swT[:], sgu_wT[l].rearrange("g q p -> q g p"), writes=["swT"])
                    S.dma("sp", sgub[:], sgu_bT[l], writes=["sgub"])
                    S.dma("sp", sgng[:], sgu_norm_g[l:l + 1, :].to_broadcast([128, 256]), writes=["sgng"])
                    S.dma("sp", qkg[:, 0:8, :], qn_g[l:l + 1, :].unsqueeze(1).to_broadcast([128, 8, 64]), writes=["qkg0"])
                    S.dma("sp", qkg[:, 8:10, :], kn_g[l:l + 1, :].unsqueeze(1).to_broadcast([128, 2, 64]), writes=["qkg1"])
                    S.dma("sp", rope[:, :, 0:16], cosd_d.rearrange("(tb p) c -> p tb c", p=128), writes=["rope0"])
                    S.dma("sp", rope[:, :, 16:32], sind_d.rearrange("(tb p) c -> p tb c", p=128), writes=["rope1"])
                    S.dma("sp", rope[:, :, 32:64], cosg_d.rearrange("(tb p) c -> p tb c", p=128), writes=["rope2"])
                    S.dma("sp", rope[:, :, 64:96], sing_d.rearrange("(tb p) c -> p tb c", p=128), writes=["rope3"])
                    S.barrier()

                    def a1_load(tb):
                        S.dma("sp", xt[tb % 2][:], x_src[tb * 128:(tb + 1) * 128, :], reads=[("xs", tb)], writes=[("xt", tb % 2)])

                    def a1_fa(tb):
                        pz = tb % 2
                        hf, hb, hT, au, av, st, sm = hfP[pz], hbP[pz], hTP[pz], auP[pz], avP[pz], stP[pz], smP[pz]
                        x_t = xt[tb % 2]
                        xk = ("xt", tb % 2)
                        S.op("dve", lambda e: e.scalar_tensor_tensor(hf[:], x_t[:], 1.0, x_t[:], ALU.mult, ALU.mult,
                                                                     accum_out=sm[:, 0:1]),
                             reads=[xk], writes=[("hf", pz), ("sm0", pz)])
                        rstd_from_ssq(sm[:, 0:1], sm[:, 2:3], 1, D, ("sm0", pz), ("sm2", pz), ("sm1", pz), sm[:, 1:2])
                        S.op("dve", lambda e: e.scalar_tensor_tensor(hf[:], x_t[:], sm[:, 2:3], mod[:, 1, :], ALU.mult, ALU.mult),
                             reads=[xk, ("sm2", pz)], writes=[("hf", pz)])
                        S.op("dve", lambda e: e.tensor_tensor(hb[:], hf[:], mod[:, 0, :], ALU.add),
                             reads=[("hf", pz)], writes=[("hb", pz)])
                        if tb + 2 < NB:
                            a1_load(tb + 2)
                        for dc in range(8):
                            S.op("pe", lambda e: e.transpose(psb(0)[:, dc * 128:(dc + 1) * 128], hb[:, dc * 128:(dc + 1) * 128], idb[:]),
                                 reads=[("hb", pz)], writes=[pb(0)])
                        S.op("act", lambda e: e.copy(hT[:].rearrange("p a b -> p (a b)"), psb(0)[:, :]),
                             reads=[], writes=[pb(0), ("hT", pz)])
                    def a1_fb(tb):
                        pz = tb % 2
                        hf, hb, hT, au, av, st, sm = hfP[pz], hbP[pz], hTP[pz], auP[pz], avP[pz], stP[pz], smP[pz]
                        for cc in range(4):
                            for dc in range(8):
                                S.op("pe", lambda e: e.matmul(ps[:, 1 + cc, :], hT[:, dc, :], wib[:, dc, cc * 512:(cc + 1) * 512],
                                                              start=(dc == 0), stop=(dc == 7)),
                                     reads=[("hT", pz)], writes=[pb(1 + cc)], inc=(dc == 7))
                        S.op("act", lambda e: e.activation(au[:], ps[:, 1, 0:256], AF.Gelu_apprx_tanh), writes=[pb(1), ("au", pz)])
                        S.op("act", lambda e: e.activation(av[:], ps[:, 1, 256:512], AF.Gelu_apprx_tanh), writes=[pb(1), ("av", pz)])
                        S.op("act", lambda e: e.copy(st[:, 0:512], ps[:, 2, :]), writes=[pb(2), ("st0", pz)])
                        S.op("act", lambda e: e.copy(st[:, 512:1024], ps[:, 3, :]), writes=[pb(3), ("st1", pz)])
                        S.op("act", lambda e: e.copy(st[:, 1024:1536], ps[:, 4, :]), writes=[pb(4), ("st2", pz)])
                    def a1_b1(tb):
                        pz = tb % 2
                        hf, hb, hT, au, av, st, sm = hfP[pz], hbP[pz], hTP[pz], auP[pz], avP[pz], stP[pz], smP[pz]
                        S.op("dve", lambda e: e.scalar_tensor_tensor(tmpa[:, 0:256], av[:], 1.0, av[:], ALU.mult, ALU.mult,
                                                                     accum_out=sm[:, 4:5]),
                             reads=[("av", pz)], writes=["tmpa", "sm4"])
                        rstd_from_ssq(sm[:, 4:5], sm[:, 6:7], 1, 256, "sm4", "sm6", "sm5", sm[:, 5:6])
                        S.op("dve", lambda e: e.scalar_tensor_tensor(avn[:], av[:], sm[:, 6:7], sgng[:], ALU.mult, ALU.mult),
                             reads=[("av", pz), "sm6"], writes=["avn"])
                        for g in range(4):
                            S.op("pe", lambda e: e.matmul(ps[:, 5, g * 64:(g + 1) * 64], swT[:, g, :], avn[:, g * 64:(g + 1) * 64],
                                                          start=True, stop=True, skip_group_check=True),
                                 reads=["avn"], writes=[pb(5)])
                        S.op("dve", lambda e: e.tensor_tensor(tmpa[:, 0:256].rearrange("p (g c) -> p g c", g=4),
                                                              ps[:, 5, 0:256].rearrange("p (g c) -> p g c", g=4),
                                                              sgub[:].unsqueeze(2).to_broadcast([128, 4, 64]), ALU.add),
                             writes=[pb(5), "tmpa"])
                        S.op("dve", lambda e: e.tensor_tensor(mixA[:, tb, :], tmpa[:, 0:256], au[:], ALU.mult),
                             reads=["tmpa", ("au", pz)], writes=[("mixA", tb)])
                    def a1_b2(tb):
                        pz = tb % 2
                        hf, hb, hT, au, av, st, sm = hfP[pz], hbP[pz], hTP[pz], auP[pz], avP[pz], stP[pz], smP[pz]
                        S.dma("sp", ndk[l, tb * 128:(tb + 1) * 128, :], st[:, 256:512], reads=[("st0", pz)], writes=[("ndk", tb)])
                        S.dma("sp", ndv[l, tb * 128:(tb + 1) * 128, :], st[:, 512:768], reads=[("st1", pz)], writes=[("ndv", tb)])
                        S.dma("sp", ngv[l, tb * 128:(tb + 1) * 128, :], st[:, 1408:1536], reads=[("st2", pz)], writes=[("ngv", tb)])
                        S.op("pool", lambda e: e.tensor_copy(Vd[:, tb, :, 0:64], st[:, 512:768].rearrange("p (h d) -> p h d", h=4)),
                             reads=[("st1", pz)], writes=[("Vd", tb)])
                        S.op("pool", lambda e: e.tensor_copy(Vg[:, tb, :, 0:64], st[:, 1408:1536].rearrange("p (h d) -> p h d", h=2)),
                             reads=[("st2", pz)], writes=[("Vg", tb)])
                        gq = st[:, 768:1408]
                        S.op("dve", lambda e: e.tensor_tensor(tmpa[:], gq, gq, ALU.mult), reads=[("st1", pz), ("st2", pz)], writes=["tmpa"])
                        S.op("dve", lambda e: e.tensor_reduce(sm[:, 8:18], tmpa[:].rearrange("p (h d) -> p h d", h=10), AX.X, ALU.add),
                             reads=["tmpa"], writes=["sm8"])
                        rstd_from_ssq(sm[:, 8:18], sm[:, 28:38], 10, 64, "sm8", "sm28", "sm18", sm[:, 18:28])
                        S.op("dve", lambda e: e.tensor_tensor(gn[:].rearrange("p (h d) -> p h d", h=10),
                                                              gq.rearrange("p (h d) -> p h d", h=10),
                                                              sm[:, 28:38].unsqueeze(2).to_broadcast([128, 10, 64]), ALU.mult),
                             reads=[("st1", pz), ("st2", pz), "sm28"], writes=["gn"])
                        S.op("dve", lambda e: e.tensor_tensor(gn[:], gn[:], qkg[:].rearrange("p h d -> p (h d)"), ALU.mult),
                             reads=[], writes=["gn"])
                        S.dma("sp", ngk[l, tb * 128:(tb + 1) * 128, :], gn[:, 512:640], reads=["gn"], writes=[("ngk", tb)])
                        dqk = st[:, 0:512].rearrange("p (g d) -> p g d", g=16)
                        cd = rope[:, tb, 0:16].unsqueeze(1).to_broadcast([128, 16, 16])
                        sd = rope[:, tb, 16:32].unsqueeze(1).to_broadcast([128, 16, 16])
                        r0 = rt[:, 0, 0:256].rearrange("p (g d) -> p g d", g=16)
                        r1 = rt[:, 1, 0:256].rearrange("p (g d) -> p g d", g=16)
                        r2 = rt[:, 2, 0:256].rearrange("p (g d) -> p g d", g=16)
                        r3 = rt[:, 3, 0:256].rearrange("p (g d) -> p g d", g=16)
                        od = qkr[:, 0:512].rearrange("p (g d) -> p g d", g=16)
                        S.op("dve", lambda e: e.tensor_tensor(r0, dqk[:, :, 0:16], cd, ALU.mult), reads=[("st0", pz)], writes=["rt0"])
                        S.op("pool", lambda e: e.tensor_tensor(r1, dqk[:, :, 16:32], sd, ALU.mult), reads=[("st0", pz)], writes=["rt1"])
                        S.op("dve", lambda e: e.tensor_tensor(r2, dqk[:, :, 16:32], cd, ALU.mult), reads=[("st0", pz)], writes=["rt2"])
                        S.op("pool", lambda e: e.tensor_tensor(r3, dqk[:, :, 0:16], sd, ALU.mult), reads=[("st0", pz)], writes=["rt3"])
                        S.op("dve", lambda e: e.tensor_tensor(od[:, :, 0:16], r0, r1, ALU.subtract), reads=["rt0", "rt1"], writes=["qkr0"])
                        S.op("dve", lambda e: e.tensor_tensor(od[:, :, 16:32], r2, r3, ALU.add), reads=["rt2", "rt3"], writes=["qkr1"])
                        gnq = gn[:, 0:512].rearrange("p (u t d) -> p u t d", u=2, t=4)
                        oq = qkr[:, 512:1024].rearrange("p (t u d) -> p u t d", t=4, u=2)
                        cg4 = rope[:, tb, 32:64].unsqueeze(1).unsqueeze(1).to_broadcast([128, 2, 4, 32])
                        sg4 = rope[:, tb, 64:96].unsqueeze(1).unsqueeze(1).to_broadcast([128, 2, 4, 32])
                        q0 = rq[:, 0, 0:256].rearrange("p (u t d) -> p u t d", u=2, t=4)
                        q1 = rq[:, 1, 0:256].rearrange("p (u t d) -> p u t d", u=2, t=4)
                        q2 = rq[:, 2, 0:256].rearrange("p (u t d) -> p u t d", u=2, t=4)
                        q3 = rq[:, 3, 0:256].rearrange("p (u t d) -> p u t d", u=2, t=4)
                        S.op("dve", lambda e: e.tensor_tensor(q0, gnq[:, :, :, 0:32], cg4, ALU.mult), reads=["gn"], writes=["rq0"])
                        S.op("pool", lambda e: e.tensor_tensor(q1, gnq[:, :, :, 32:64], sg4, ALU.mult), reads=["gn"], writes=["rq1"])
                        S.op("dve", lambda e: e.tensor_tensor(q2, gnq[:, :, :, 32:64], cg4, ALU.mult), reads=["gn"], writes=["rq2"])
                        S.op("pool", lambda e: e.tensor_tensor(q3, gnq[:, :, :, 0:32], sg4, ALU.mult), reads=["gn"], writes=["rq3"])
                        S.op("dve", lambda e: e.tensor_tensor(oq[:, :, :, 0:32], q0, q1, ALU.subtract), reads=["rq0", "rq1"], writes=["qkr2"])
                        S.op("dve", lambda e: e.tensor_tensor(oq[:, :, :, 32:64], q2, q3, ALU.add), reads=["rq2", "rq3"], writes=["qkr3"])
                        gnk = gn[:, 512:640].rearrange("p (h d) -> p h d", h=2)
                        ok_ = qkr[:, 1024:1152].rearrange("p (h d) -> p h d", h=2)
                        cg2 = rope[:, tb, 32:64].unsqueeze(1).to_broadcast([128, 2, 32])
                        sg2 = rope[:, tb, 64:96].unsqueeze(1).to_broadcast([128, 2, 32])
                        k0 = rt[:, 0, 256:320].rearrange("p (h d) -> p h d", h=2)
                        k1 = rt[:, 1, 256:320].rearrange("p (h d) -> p h d", h=2)
                        k2 = rt[:, 2, 256:320].rearrange("p (h d) -> p h d", h=2)
                        k3 = rt[:, 3, 256:320].rearrange("p (h d) -> p h d", h=2)
                        S.op("dve", lambda e: e.tensor_tensor(k0, gnk[:, :, 0:32], cg2, ALU.mult), reads=["gn"], writes=["rk0"])
                        S.op("pool", lambda e: e.tensor_tensor(k1, gnk[:, :, 32:64], sg2, ALU.mult), reads=["gn"], writes=["rk1"])
                        S.op("dve", lambda e: e.tensor_tensor(k2, gnk[:, :, 32:64], cg2, ALU.mult), reads=["gn"], writes=["rk2"])
                        S.op("pool", lambda e: e.tensor_tensor(k3, gnk[:, :, 0:32], sg2, ALU.mult), reads=["gn"], writes=["rk3"])
                        S.op("dve", lambda e: e.tensor_tensor(ok_[:, :, 0:32], k0, k1, ALU.subtract), reads=["rk0", "rk1"], writes=["qkr4"])
                        S.op("dve", lambda e: e.tensor_tensor(ok_[:, :, 32:64], k2, k3, ALU.add), reads=["rk2", "rk3"], writes=["qkr5"])
                        qkeys = ["qkr0", "qkr1", "qkr2", "qkr3", "qkr4", "qkr5"]
                        for j in range(8):
                            S.op("pe", lambda e: e.transpose(psb(6)[:, j * 128:(j + 1) * 128], qkr[:, j * 128:(j + 1) * 128], idb[:]),
                                 reads=qkeys, writes=[pb(6)])
                        S.op("pe", lambda e: e.transpose(psb(7)[:, 0:128], qkr[:, 1024:1152], idb[:]), reads=qkeys, writes=[pb(7)])
                        tsl = slice(tb * 128, (tb + 1) * 128)
                        S.op("act", lambda e: e.copy(QTd[:, :, tsl], psb(6)[:, 0:256].rearrange("p (a b) -> p a b", a=2)),
                             writes=[pb(6), ("QTd", tb)])
                        S.op("dve", lambda e: e.tensor_copy(KTd[:, :, tsl], psb(6)[:, 256:512].rearrange("p (a b) -> p a b", a=2)),
                             writes=[pb(6), ("KTd", tb)])
                        S.op("act", lambda e: e.copy(QTg[:, :, tsl], psb(6)[:, 512:1024].rearrange("p (a b) -> p a b", a=4)),
                             writes=[pb(6), ("QTg", tb)])
                        S.op("dve", lambda e: e.tensor_copy(KTg[:, tsl], psb(7)[:, 0:128]), writes=[pb(7), ("KTg", tb)])

                    a1_load(0)
                    a1_load(1)
                    a1_fa(0)
                    a1_fb(0)
                    for tb in range(NB):
                        if tb + 1 < NB:
                            a1_fa(tb + 1)
                        a1_b1(tb)
                        if tb + 1 < NB:
                            a1_fb(tb + 1)
                        a1_b2(tb)
                    S.dma("pool", cst4[:, :, 0:256], cdk[l].rearrange("(cb p) c -> p cb c", p=128), writes=["cst0"])
                    S.dma("pool", cst4[:, :, 256:384], cgk[l].rearrange("(cb p) c -> p cb c", p=128), writes=["cst1"])
                    S.dma("pool", cst4[:, :, 384:640], cdv[l].rearrange("(cb p) c -> p cb c", p=128), writes=["cst2"])
                    S.dma("pool", cst4[:, :, 640:768], cgv[l].rearrange("(cb p) c -> p cb c", p=128), writes=["cst3"])
                    for cb in range(4):
                        kb = NB + cb
                        cst = cst4[:, cb, :]
                        for j in range(3):
                            S.op("pe", lambda e: e.transpose(psb(6)[:, j * 128:(j + 1) * 128], cst[:, j * 128:(j + 1) * 128], idb[:]),
                                 reads=["cst0", "cst1"], writes=[pb(6)])
                        ksl = slice(kb * 128, (kb + 1) * 128)
                        S.op("act", lambda e: e.copy(KTd[:, :, ksl], psb(6)[:, 0:256].rearrange("p (a b) -> p a b", a=2)),
                             writes=[pb(6), ("KTd", kb)])
                        S.op("dve", lambda e: e.tensor_copy(KTg[:, ksl], psb(6)[:, 256:384]), writes=[pb(6), ("KTg", kb)])
                        S.op("pool", lambda e: e.tensor_copy(Vd[:, kb, :, 0:64], cst[:, 384:640].rearrange("p (h d) -> p h d", h=4)),
                             reads=["cst2"], writes=[("Vd", kb)])
                        S.op("pool", lambda e: e.tensor_copy(Vg[:, kb, :, 0:64], cst[:, 640:768].rearrange("p (h d) -> p h d", h=2)),
                             reads=["cst3"], writes=[("Vg", kb)])
                    S.barrier()
                if stop_after == ("A1", l):
                    S.barrier()
                    return nc

                with ExitStack() as P:
                    def sb(name, shape, dt=F32):
                        return P.enter_context(nc.sbuf_tensor(f"{name}_{l}", list(shape), dt))
                    wob = sb("wob", [128, 8, D], BF16)
                    slg = sb("slg", [128, 64])
                    PT = [sb(f"PT{i}", [128, 512], BF16) for i in range(4)]
                    mixBC = sb("mixBC", [128, 4, 768], BF16)
                    od2 = sb("od2", [128, 2, 4, 64])
                    rz = sb("rz", [128, 2, 4])
                    osb = sb("osb", [128, 4, 64]); osq = sb("osq", [128, 4, 64])
                    ss4 = sb("ss4", [128, 12])
                    mixT = sb("mixT", [128, 8, 128], BF16)
                    xt = [sb(f"xa{i}", [128, D]) for i in range(2)]
                    xo = [sb(f"xo{i}", [128, D]) for i in range(2)]
                    wov = w_out[l].rearrange("(dc p) n -> p dc n", p=128)
                    S.dma("pool", wob[:], wov, writes=["wob"])
                    S.dma("sp", slg[:], subln_g[l:l + 1, :].to_broadcast([128, 64]), writes=["slg"])
                    bg_toks = []
                    utf = ut[l].rearrange("i p a b -> (i p) (a b)")
                    utbf = utb_l[l].rearrange("i p n -> (i p) n")
                    vtbf = vtb_l[l].rearrange("i p n -> (i p) n")
                    for c8 in range(2):
                        rs8 = slice(c8 * 8192, (c8 + 1) * 8192)
                        bg_toks.append(S.bg_dma("pool", utbf[rs8, :], utf[rs8, :]))
                        bg_toks.append(S.bg_dma("pool", vtbf[rs8, :], vt[l, rs8, :]))
                    S.op("dve", lambda e: e.tensor_scalar(slg[:], slg[:], 1.0 - lam_init, None, ALU.mult), writes=["slg"])
                    S.barrier()
                    LA = 2
                    for qc in range(4):
                        qsl = slice(qc * 512, (qc + 1) * 512)

                        def map_cfg(m):
                            if m < 8:
                                j, r0, K = m // 4, 32 * (m % 4), 32
                                return dict(r0=r0, K=K, scale=DIFF_SCALE,
                                            kt=lambda kb: KTd[r0:r0 + K, j, kb * 128:(kb + 1) * 128],
                                            qt=QTd[r0:r0 + K, j, qsl],
                                            vv=lambda kb: Vd[:, kb, m // 2, :])
                            g = m - 8
                            j, r0, K = g % 4, 64 * (g // 4), 64
                            return dict(r0=r0, K=K, scale=GQA_SCALE,
                                        kt=lambda kb: KTg[r0:r0 + K, kb * 128:(kb + 1) * 128],
                                        qt=QTg[r0:r0 + K, j, qsl],
                                        vv=lambda kb: Vg[:, kb, g // 4, :])

                        cfgs = [map_cfg(m) for m in range(16)]
                        steps = [(m, kb) for m in range(16) for kb in range(NKB)]

                        def emit_st(n):
                            m, kb = steps[n]
                            c = cfgs[m]
                            sbk = n % 3
                            S.op("pe", lambda e: e.matmul(ps[:, sbk, :], c["kt"](kb), c["qt"], start=True, stop=True,
                                                          tile_position=(c["r0"], 0)),
                                 reads=[], writes=[pb(sbk)])
                            pt = PT[n % 4]
                            for hq in range(2):
                                col = kb * 8 + qc * 2 + hq
                                S.op("act", lambda e: e.activation(pt[:, hq * 256:(hq + 1) * 256], ps[:, sbk, hq * 256:(hq + 1) * 256],
                                                                   AF.Exp, bias=maskb[:, col:col + 1], scale=c["scale"]),
                                     reads=[], writes=[pb(sbk), ("PT", n % 4, hq)])

                        def emit_pv(n):
                            m, kb = steps[n]
                            c = cfgs[m]
                            pvb = 3 + (m % 2)
                            pt = PT[n % 4]
                            for qs in range(4):
                                S.op("pe", lambda e: e.matmul(ps[:, pvb, qs * 65:(qs + 1) * 65], pt[:, qs * 128:(qs + 1) * 128], c["vv"](kb),
                                                              start=(kb == 0 and qs == 0), stop=(kb == NKB - 1),
                                                              skip_group_check=True),
                                     reads=[("PT", n % 4, qs // 2)], writes=[pb(pvb)], inc=(qs == 3))
                            if kb != NKB - 1:
                                return
                            pv = ps[:, pvb, 0:260].rearrange("p (q c) -> p q c", q=4)
                            if m < 8:
                                xh = m % 2
                                S.op("dve", lambda e: e.reciprocal(rz[:, xh, :], pv[:, :, 64]), writes=[pb(pvb), ("rz", xh)])
                                S.op("dve", lambda e: e.tensor_tensor(od2[:, xh, :, :], pv[:, :, 0:64],
                                                                      rz[:, xh, :].unsqueeze(2).to_broadcast([128, 4, 64]), ALU.mult),
                                     reads=[("rz", xh)], writes=[pb(pvb), ("od2", xh)])
                                if xh == 1:
                                    h = m // 2
                                    S.op("dve", lambda e: e.scalar_tensor_tensor(osb[:].rearrange("p a b -> p (a b)"),
                                                                                 od2[:, 1, :, :].rearrange("p a b -> p (a b)"), neglam[:, 0:1],
                                                                                 od2[:, 0, :, :].rearrange("p a b -> p (a b)"), ALU.mult, ALU.add),
                                         reads=[("od2", 0), ("od2", 1)], writes=["osb"])
                                    S.op("dve", lambda e: e.tensor_tensor(osq[:], osb[:], osb[:], ALU.mult), reads=["osb"], writes=["osq"])
                                    S.op("dve", lambda e: e.tensor_reduce(ss4[:, 0:4], osq[:], AX.X, ALU.add), reads=["osq"], writes=["ss4a"])
                                    rstd_from_ssq(ss4[:, 0:4], ss4[:, 8:12], 4, 64, "ss4a", "ss4c", "ss4b", ss4[:, 4:8])
                                    S.op("dve", lambda e: e.tensor_tensor(osq[:], osb[:], ss4[:, 8:12].unsqueeze(2).to_broadcast([128, 4, 64]), ALU.mult),
                                         reads=["osb", "ss4c"], writes=["osq"])
                                    S.op("dve", lambda e: e.tensor_tensor(mixBC[:, :, h * 64:(h + 1) * 64], osq[:],
                                                                          slg[:].unsqueeze(1).to_broadcast([128, 4, 64]), ALU.mult),
                                         reads=["osq"], writes=[("mixBC", m)])
                            else:
                                g = m - 8
                                S.op("dve", lambda e: e.reciprocal(rz[:, 0, :], pv[:, :, 64]), writes=[pb(pvb), ("rz", 0)])
                                S.op("dve", lambda e: e.tensor_tensor(mixBC[:, :, 256 + g * 64:256 + (g + 1) * 64], pv[:, :, 0:64],
                                                                      rz[:, 0, :].unsqueeze(2).to_broadcast([128, 4, 64]), ALU.mult),
                                     reads=[("rz", 0)], writes=[pb(pvb), ("mixBC", m)])

                        for n in range(len(steps) + LA):
                            if n < len(steps):
                                emit_st(n)
                            if n - LA >= 0:
                                emit_pv(n - LA)
                        mkeys = [("mixBC", m) for m in range(16)]
                        for qs in range(4):
                            tb = qc * 4 + qs
                            x_t = xt[tb % 2]; xk = ("xa", tb % 2)
                            x_o = xo[tb % 2]; ok2 = ("xo", tb % 2)
                            S.dma("sp", x_t[:], x_src[tb * 128:(tb + 1) * 128, :], reads=[("xs", tb)], writes=[xk])
                            for dc in range(8):
                                src = mixA[:, tb, dc * 128:(dc + 1) * 128] if dc < 2 else mixBC[:, qs, (dc - 2) * 128:(dc - 1) * 128]
                                S.op("pe", lambda e: e.transpose(psb(5)[:, dc * 128:(dc + 1) * 128], src, idb[:]),
                                     reads=mkeys + [("mixA", tb)], writes=[pb(5)])
                            S.op("act", lambda e: e.copy(mixT[:].rearrange("p a b -> p (a b)"), psb(5)[:, :]), writes=[pb(5), "mixT"])
                            for cc in range(2):
                                for dc in range(8):
                                    S.op("pe", lambda e: e.matmul(ps[:, 6 + cc, :], mixT[:, dc, :], wob[:, dc, cc * 512:(cc + 1) * 512],
                                                                  start=(dc == 0), stop=(dc == 7)),
                                         reads=["mixT"], writes=[pb(6 + cc)], inc=(dc == 7))
                            for cc in range(2):
                                csl = slice(cc * 512, (cc + 1) * 512)
                                S.op("dve", lambda e: e.tensor_tensor(x_o[:, csl], ps[:, 6 + cc, :], mod[:, 2, csl], ALU.mult),
                                     writes=[pb(6 + cc), ok2])
                            S.op("pool", lambda e: e.tensor_tensor(x_o[:], x_o[:], x_t[:], ALU.add), reads=[xk], writes=[ok2])
                            S.dma("sp", xs[tb * 128:(tb + 1) * 128, :], x_o[:], reads=[ok2], writes=[("xs", tb)])
                    S.barrier()
            if stop_after == ("A3", l):
                S.barrier()
                return nc

            with ExitStack() as PP:
                def sbp(name, shape, dt=F32):
                    return PP.enter_context(nc.sbuf_tensor(f"{name}_{l}", list(shape), dt))
                with ExitStack() as P:
                    def sb(name, shape, dt=F32):
                        return P.enter_context(nc.sbuf_tensor(f"{name}_{l}", list(shape), dt))
                    wqb = sb("wqb", [128, 8, D], BF16)
                    skb = sb("skb", [128, 256], BF16)
                    xt = [sb(f"xp{i}", [128, D]) for i in range(2)]
                    hfP = [sb(f"hf2{i}", [128, D]) for i in range(2)]
                    hbP = [sb(f"hb2{i}", [128, D], BF16) for i in range(2)]
                    hTP = [sb(f"hT2{i}", [128, 8, 128], BF16) for i in range(2)]
                    qTP = [sb(f"qT{i}", [128, 8, 128], BF16) for i in range(2)]
                    smP = [sb(f"smp{i}", [128, 16]) for i in range(2)]
                    scP = [sb(f"sc{i}", [128, 2048]) for i in range(2)]
                    sc2 = sb("sc2", [128, 2048])
                    sv = sb("sv", [128, 16, 16]); si = sb("si", [128, 16, 16], U32); sif = sb("sif", [128, 16, 16])
                    cand = sb("cand", [128, 8, 256]); cand2 = sb("cand2", [128, 8, 256])
                    tsv = sb("tsv", [128, 8, 16]); tiu = sb("tiu", [128, 8, 16], U32)
                    au32 = sb("au32", [128, 128], U32); bu32 = sb("bu32", [128, 128], U32)
                    af_ = sb("af_", [128, 128]); bf_ = sb("bf_", [128, 128])
                    oh = sb("oh", [128, 2048]); IJ = sb("IJ", [128, 2, 128])
                    ex = sb("ex", [128, 8, 16]); zz = sb("zz", [128, 16]); gate = sb("gate", [128, 128])
                    ijs = sb("ijs", [128, 384])
                    wqv = wq[l].rearrange("(dc p) n -> p dc n", p=128)
                    S.dma("pool", wqb[:], wqv, writes=["wqb"])
                    S.dma("pool", skb[:], skbd[l], writes=["skb"])
                    S.barrier()
                    def p1_load(tb):
                        S.dma("sp", xt[tb % 2][:], xs[tb * 128:(tb + 1) * 128, :], reads=[("xs", tb)], writes=[("xp", tb % 2)])

                    def p1_front(tb):
                        pz = tb % 2
                        hf, hb, hT, qT, sm, sc = hfP[pz], hbP[pz], hTP[pz], qTP[pz], smP[pz], scP[pz]
                        tsl = slice(tb * 128, (tb + 1) * 128)
                        x_t = xt[tb % 2]; xk = ("xp", tb % 2)
                        tsl = slice(tb * 128, (tb + 1) * 128)
                        S.op("dve", lambda e: e.scalar_tensor_tensor(hf[:], x_t[:], 1.0, x_t[:], ALU.mult, ALU.mult, accum_out=sm[:, 0:1]),
                             reads=[xk], writes=[("hf", pz), ("sm0", pz)])
                        rstd_from_ssq(sm[:, 0:1], sm[:, 2:3], 1, D, ("sm0", pz), ("sm2", pz), ("sm1", pz), sm[:, 1:2])
                        S.op("dve", lambda e: e.scalar_tensor_tensor(hf[:], x_t[:], sm[:, 2:3], mod[:, 4, :], ALU.mult, ALU.mult),
                             reads=[xk, ("sm2", pz)], writes=[("hf", pz)])
                        S.op("dve", lambda e: e.tensor_tensor(hb[:], hf[:], mod[:, 3, :], ALU.add), reads=[("hf", pz)], writes=[("hb", pz)])
                        if tb + 2 < NB:
                            p1_load(tb + 2)
                        for dc in range(8):
                            S.op("pe", lambda e: e.transpose(psb(0)[:, dc * 128:(dc + 1) * 128], hb[:, dc * 128:(dc + 1) * 128], idb[:]),
                                 reads=[("hb", pz)], writes=[pb(0)])
                        S.op("act", lambda e: e.copy(hT[:].rearrange("p a b -> p (a b)"), psb(0)[:, :]), writes=[pb(0), ("hT", pz)])
                        S.dma("sp", h2s[tb], hT[:].rearrange("p a b -> p (a b)"), reads=[("hT", pz)], writes=[("h2s", tb)])
                        for hh in range(8):
                            bk = 1 + hh // 4
                            for dc in range(8):
                                S.op("pe", lambda e: e.matmul(ps[:, bk, (hh % 4) * 128:(hh % 4 + 1) * 128], wqb[:, dc, hh * 128:(hh + 1) * 128],
                                                              hT[:, dc, :], start=(dc == 0), stop=(dc == 7), skip_group_check=True),
                                     reads=[("hT", pz)], writes=[pb(bk)])
                        S.op("act", lambda e: e.copy(qT[:, 0:4, :].rearrange("p a b -> p (a b)"), ps[:, 1, :]), writes=[pb(1), ("qTa", pz)])
                        S.op("act", lambda e: e.copy(qT[:, 4:8, :].rearrange("p a b -> p (a b)"), ps[:, 2, :]), writes=[pb(2), ("qTb", pz)])
                        for hh in range(8):
                            bk = 3 + hh // 2
                            S.op("pe", lambda e: e.matmul(ps[:, bk, (hh % 2) * 256:(hh % 2 + 1) * 256], qT[:, hh, :], skb[:],
                                                          start=True, stop=True, skip_group_check=True),
                                 reads=[("qTa", pz), ("qTb", pz)], writes=[pb(bk)])
                        for b4 in range(4):
                            eng = "act"
                            if eng == "act":
                                S.op("act", lambda e: e.copy(sc[:, b4 * 512:(b4 + 1) * 512], ps[:, 3 + b4, :]), writes=[pb(3 + b4), ("sc", pz, b4)])
                            else:
                                S.op("dve", lambda e: e.tensor_copy(sc[:, b4 * 512:(b4 + 1) * 512], ps[:, 3 + b4, :]), writes=[pb(3 + b4), ("sc", pz, b4)])
                    def p1_back(tb):
                        pz = tb % 2
                        hf, hb, hT, qT, sm, sc = hfP[pz], hbP[pz], hTP[pz], qTP[pz], smP[pz], scP[pz]
                        tsl = slice(tb * 128, (tb + 1) * 128)
                        GS = [slice(gp * 128, (gp + 1) * 128) for gp in range(16)]
                        for gp in range(16):
                            S.op("dve", lambda e: e.max(sv[:, gp, 0:8], sc[:, GS[gp]]), reads=[("sc", pz, gp // 4)], writes=[("sv", gp)])
                        for gp in range(16):
                            S.op("dve", lambda e: e.max_index(si[:, gp, 0:8], sv[:, gp, 0:8], sc[:, GS[gp]]), reads=[("sc", pz, gp // 4), ("sv", gp)], writes=[("si", gp)])
                        for gp in range(16):
                            S.op("dve", lambda e: e.match_replace(sc2[:, GS[gp]], sv[:, gp, 0:8], sc[:, GS[gp]], -1e30), reads=[("sc", pz, gp // 4), ("sv", gp)], writes=[("sc2", gp)])
                        for gp in range(16):
                            S.op("dve", lambda e: e.max(sv[:, gp, 8:16], sc2[:, GS[gp]]), reads=[("sc2", gp)], writes=[("svb", gp)])
                        for gp in range(16):
                            S.op("dve", lambda e: e.max_index(si[:, gp, 8:16], sv[:, gp, 8:16], sc2[:, GS[gp]]), reads=[("sc2", gp), ("svb", gp)], writes=[("sib", gp)])
                        allsv = [("sv", g_) for g_ in range(16)] + [("svb", g_) for g_ in range(16)]
                        allsi = [("si", g_) for g_ in range(16)] + [("sib", g_) for g_ in range(16)]
                        S.op("dve", lambda e: e.tensor_copy(sif[:], si[:]), reads=allsi, writes=["sif"])
                        sv4 = sv[:].rearrange("p (h x) k -> p h x k", x=2)
                        S.op("dve", lambda e: e.tensor_tensor(cand[:].rearrange("p h (a b) -> p h a b", a=16),
                                                              sv4[:, :, 0, :].unsqueeze(3).to_broadcast([128, 8, 16, 16]),
                                                              sv4[:, :, 1, :].unsqueeze(2).to_broadcast([128, 8, 16, 16]), ALU.add),
                             reads=allsv, writes=["cand"])
                        for hh in range(8):
                            S.op("dve", lambda e: e.max(tsv[:, hh, 0:8], cand[:, hh, :]), reads=["cand"], writes=[("tsv", hh)])
                        for hh in range(8):
                            S.op("dve", lambda e: e.max_index(tiu[:, hh, 0:8], tsv[:, hh, 0:8], cand[:, hh, :]), reads=["cand", ("tsv", hh)], writes=[("tiu", hh)])
                        for hh in range(8):
                            S.op("dve", lambda e: e.match_replace(cand2[:, hh, :], tsv[:, hh, 0:8], cand[:, hh, :], -1e30), reads=["cand", ("tsv", hh)], writes=[("cand2", hh)])
                        for hh in range(8):
                            S.op("dve", lambda e: e.max(tsv[:, hh, 8:16], cand2[:, hh, :]), reads=[("cand2", hh)], writes=[("tsvb", hh)])
                        for hh in range(8):
                            S.op("dve", lambda e: e.max_index(tiu[:, hh, 8:16], tsv[:, hh, 8:16], cand2[:, hh, :]), reads=[("cand2", hh), ("tsvb", hh)], writes=[("tiub", hh)])
                        alltiu = [("tiu", h_) for h_ in range(8)] + [("tiub", h_) for h_ in range(8)]
                        alltsv = [("tsv", h_) for h_ in range(8)] + [("tsvb", h_) for h_ in range(8)]
                        tiuf = tiu[:].rearrange("p h k -> p (h k)")
                        S.op("dve", lambda e: e.tensor_single_scalar(au32[:], tiuf, c4u[:, 0:1], ALU.logical_shift_right), reads=alltiu, writes=["au32"])
                        S.op("dve", lambda e: e.tensor_single_scalar(bu32[:], tiuf, c15u[:, 0:1], ALU.bitwise_and), reads=alltiu, writes=["bu32"])
                        S.op("dve", lambda e: e.tensor_copy(af_[:], au32[:]), reads=["au32"], writes=["af_"])
                        S.op("dve", lambda e: e.tensor_copy(bf_[:], bu32[:]), reads=["bu32"], writes=["bf_"])
                        sif4 = sif[:].rearrange("p (h x) k -> p h x k", x=2)
                        for xx, sel in ((0, af_), (1, bf_)):
                            S.op("dve", lambda e: e.tensor_tensor(oh[:].rearrange("p (n a) -> p n a", a=16),
                                                                  iota16[:].unsqueeze(1).to_broadcast([128, 128, 16]),
                                                                  sel[:].unsqueeze(2).to_broadcast([128, 128, 16]), ALU.is_equal),
                                 reads=["af_", "bf_"], writes=["oh"])
                            S.op("dve", lambda e: e.tensor_tensor(oh[:].rearrange("p (h k a) -> p h k a", h=8, k=16),
                                                                  oh[:].rearrange("p (h k a) -> p h k a", h=8, k=16),
                                                                  sif4[:, :, xx, :].unsqueeze(2).to_broadcast([128, 8, 16, 16]), ALU.mult),
                                 reads=["sif"], writes=["oh"])
                            S.op("dve", lambda e: e.tensor_reduce(IJ[:, xx, :], oh[:].rearrange("p (n a) -> p n a", a=16), AX.X, ALU.add),
                                 reads=["oh"], writes=[("IJ", xx)])
                        S.op("dve", lambda e: e.tensor_tensor(ex[:], tsv[:], tsv[:, :, 0:1].to_broadcast([128, 8, 16]), ALU.subtract),
                             reads=alltsv, writes=["ex"])
                        S.op("act", lambda e: e.activation(ex[:], ex[:], AF.Exp), writes=["ex"])
                        S.op("dve", lambda e: e.tensor_reduce(zz[:, 0:8], ex[:], AX.X, ALU.add), reads=["ex"], writes=["zz0"])
                        S.op("dve", lambda e: e.reciprocal(zz[:, 8:16], zz[:, 0:8]), reads=["zz0"], writes=["zz1"])
                        S.op("dve", lambda e: e.tensor_tensor(gate[:].rearrange("p (h k) -> p h k", h=8), ex[:],
                                                              zz[:, 8:16].unsqueeze(2).to_broadcast([128, 8, 16]), ALU.mult),
                             reads=["ex", "zz1"], writes=["gate"])
                        S.op("pe", lambda e: e.transpose(ps[:, 7, 0:128], IJ[:, 0, :], idf[:]), reads=[("IJ", 0)], writes=[pb(7)])
                        S.op("pe", lambda e: e.transpose(ps[:, 7, 128:256], IJ[:, 1, :], idf[:]), reads=[("IJ", 1)], writes=[pb(7)])
                        S.op("pe", lambda e: e.transpose(ps[:, 7, 256:384], gate[:], idf[:]), reads=["gate"], writes=[pb(7)])
                        S.op("act", lambda e: e.copy(ijs[:], ps[:, 7, 0:384]), writes=[pb(7), "ijs"])
                        S.dma("sp", ijg.rearrange("k p t -> p k t")[:, :, tsl], ijs[:].rearrange("p (k t) -> p k t", k=3),
                              reads=["ijs"], writes=[("ijg", tb)])
                    p1_load(0)
                    p1_load(1)
                    p1_front(0)
                    for tb in range(NB):
                        if tb + 1 < NB:
                            p1_front(tb + 1)
                        p1_back(tb)
                    S.op("act", lambda e: e.copy(g2t[:], mod[:, 5, :]), writes=["g2t"])
                    S.barrier()
                if stop_after == ("P1", l):
                    S.barrier()
                    return nc

                PM.close()
                with ExitStack() as P:
                    def sb(name, shape, dt=F32):
                        return P.enter_context(nc.sbuf_tensor(f"{name}_{l}", list(shape), dt))
                    nsub = TB // 128
                    GTs = [sb(f"GT{i}", [128, TB, 128], BF16) for i in range(2)]
                    h2Ts = [sb(f"h2T{i}", [128, nsub, 8 * 128], BF16) for i in range(2)]
                    ijts = [sb(f"ijt{i}", [128, 3, TB]) for i in range(2)]
                    NOH = 4
                    OI = [sb(f"OI{i}", [128, 4, 128], BF16) for i in range(NOH)]
                    OJ = [sb(f"OJ{i}", [128, 4, 128], BF16) for i in range(NOH)]
                    NUB = 4
                    UB = [sb(f"UB{i}", [128, 8, 128], BF16) for i in range(NUB)]
                    VB = [sb(f"VB{i}", [128, D], BF16) for i in range(NUB)]
                    NAC = 3
                    AC = [sb(f"AC{i}", [128, TB], BF16) for i in range(NAC)]
                    WB = [sb(f"WB{i}", [128, TB], BF16) for i in range(NAC)]
                    xt = [sb("xq0", [128, D])] * 2
                    xo = [sb("xr0", [128, D])] * 2
                    yo = [sb("yo0", [128, D])] * 2
                    sm = sb("smq", [128, 8])
                    utb = utb_l[l]; vtb = vtb_l[l]
                    for tk in bg_toks:
                        S.wait_bg("sp", tk)
                    last = (l == n_layers - 1)
                    npass = NTOK // TB if p2_passes is None else p2_passes
                    NCH = p2_chunks
                    LA2 = 2
                    ijv = ijg.rearrange("k p t -> p k t")
                    gctr = [0]

                    def load_pass(pp):
                        z = pp % 2
                        for s_ in range(nsub):
                            tb = pp * nsub + s_
                            S.dma("sp", h2Ts[z][:, s_, :], h2s[tb], reads=[("h2s", tb)], writes=[("h2T", z, s_)])
                        S.dma("sp", ijts[z][:], ijv[:, :, pp * TB:(pp + 1) * TB],
                              reads=[("ijg", pp * nsub + s_) for s_ in range(nsub)], writes=[("ijt", z)])

                    def g_dve(pp, t4):
                        z = pp % 2
                        ijt = ijts[z]
                        sl = t4 % NOH
                        o_i = OI[sl]; o_j = OJ[sl]
                        t0 = t4 * 4
                        for u in range(4):
                            S.op("dve", lambda e: e.tensor_scalar(o_i[:, u, :], iota_b[:], ijt[:, 0, t0 + u:t0 + u + 1],
                                                                  ijt[:, 2, t0 + u:t0 + u + 1], ALU.is_equal, ALU.mult),
                                 reads=[("ijt", z)], writes=[("OI", sl, u)])
                        S.op("dve", lambda e: e.tensor_tensor(o_j[:], iota_b[:].unsqueeze(1).to_broadcast([128, 4, 128]),
                                                              ijt[:, 1, t0:t0 + 4].unsqueeze(2).to_broadcast([128, 4, 128]), ALU.is_equal),
                             reads=[("ijt", z)], writes=[("OJ", sl)])

                    def g_pe(pp, t4):
                        sl = t4 % NOH
                        o_i = OI[sl]; o_j = OJ[sl]
                        for u in range(4):
                            S.op("pe", lambda e: e.matmul(ps[:, 7, u * 128:(u + 1) * 128], o_j[:, u, :], o_i[:, u, :],
                                                          start=True, stop=True, skip_group_check=True),
                                 reads=[("OI", sl, u), ("OJ", sl)], writes=[pb(7)], inc=(u == 3))

                    def g_act(pp, t4):
                        z = pp % 2
                        t0 = t4 * 4
                        S.op("act", lambda e: e.copy(GTs[z][:, t0:t0 + 4, :].rearrange("p t i -> p (t i)"), ps[:, 7, :]),
                             writes=[pb(7), ("GT", z)])

                    def g_half(pp, t4):
                        g_dve(pp, t4); g_pe(pp, t4); g_act(pp, t4)

                    def emit_load(pp, i):
                        ub = UB[i % NUB]; vb = VB[i % NUB]
                        ubf = ub[:].rearrange("p a b -> p (a b)")
                        S.dma("sp", ubf, utb[i], writes=[("UB", i % NUB)])
                        S.dma("sp", vb[:], vtb[i], writes=[("VB", i % NUB)])

                    def emit_a(pp, i):
                        z = pp % 2
                        ub = UB[i % NUB]
                        ab = 4 + (i % 3)
                        for dc in range(8):
                            S.op("pe", lambda e: e.matmul(ps[:, ab, 0:TB].rearrange("p (s t) -> p s t", s=nsub), ub[:, dc, :],
                                                          h2Ts[z][:, :, dc * 128:(dc + 1) * 128],
                                                          start=(dc == 0), stop=(dc == 7)),
                                 reads=[("UB", i % NUB)] + [("h2T", z, s_) for s_ in range(nsub)], writes=[pb(ab)], inc=(dc == 7))
                        ac = AC[i % NAC]; wb = WB[i % NAC]
                        S.op("act", lambda e: e.activation(ac[:], ps[:, ab, 0:TB], AF.Gelu_apprx_tanh), writes=[pb(ab), ("AC", i % NAC)])
                        S.op("dve", lambda e: e.tensor_tensor(wb[:], ac[:], GTs[z][:, :, i], ALU.mult),
                             reads=[("AC", i % NAC), ("GT", z)], writes=[("WB", i % NAC)])

                    def emit_v(pp, i):
                        vb = VB[i % NUB]; wb = WB[i % NAC]
                        for s_ in range(nsub):
                            for dh in range(2):
                                S.op("pe", lambda e: e.matmul(ps[:, 2 * s_ + dh, :], wb[:, s_ * 128:(s_ + 1) * 128], vb[:, dh * 512:(dh + 1) * 512],
                                                              start=(i == 0), stop=(i == NCH - 1)),
                                     reads=[("WB", i % NAC), ("VB", i % NUB)], writes=[pb(2 * s_ + dh)], inc=(s_ == nsub - 1 and dh == 1))

                    load_pass(0)
                    for t4 in range(TB // 4):
                        g_half(0, t4)
                    PF = NUB - 1
                    for pp in range(npass):
                        if pp + 1 < npass:
                            load_pass(pp + 1)
                        for i in range(min(PF, NCH)):
                            emit_load(pp, i)
                        NH = TB // 4
                        for n in range(NCH + LA2 + 4):
                            if n < NCH:
                                emit_a(pp, n)
                            if 0 <= n - LA2 < NCH:
                                emit_v(pp, n - LA2)
                                if n - LA2 + PF < NCH:
                                    emit_load(pp, n - LA2 + PF)
                            if pp + 1 < npass:
                                if n % 2 == 0 and n // 2 < NH:
                                    g_dve(pp + 1, n // 2)
                                if n >= 2 and n % 2 == 0 and (n - 2) // 2 < NH:
                                    g_pe(pp + 1, (n - 2) // 2)
                                if n >= 3 and n % 2 == 1 and (n - 3) // 2 < NH:
                                    g_act(pp + 1, (n - 3) // 2)
                        for s_ in range(nsub):
                            tb = pp * nsub + s_
                            tsl = slice(tb * 128, (tb + 1) * 128)
                            x_t = xt[0]; xk = ("xq", 0)
                            x_o = xo[0]; ok2 = ("xr", 0)
                            S.dma("sp", x_t[:], xs[tsl, :], reads=[("xs", tb)], writes=[xk])
                            for dh in range(2):
                                csl = slice(dh * 512, (dh + 1) * 512)
                                S.op("dve", lambda e: e.tensor_tensor(x_o[:, csl], ps[:, 2 * s_ + dh, :], g2t[:, csl], ALU.mult),
                                     writes=[pb(2 * s_ + dh), ok2])
                            S.op("pool", lambda e: e.tensor_tensor(x_o[:], x_o[:], x_t[:], ALU.add), reads=[xk], writes=[ok2])
                            if not last:
                                S.dma("sp", xs[tsl, :], x_o[:], reads=[ok2], writes=[("xs", tb)])
                            else:
                                y_o = yo[0]; yk = ("yo", 0)
                                S.op("dve", lambda e: e.scalar_tensor_tensor(y_o[:], x_o[:], 1.0, x_o[:], ALU.mult, ALU.mult, accum_out=sm[:, 0:1]),
                                     reads=[ok2], writes=[yk, "smq0"])
                                rstd_from_ssq(sm[:, 0:1], sm[:, 2:3], 1, D, "smq0", "smq2", "smq1", sm[:, 1:2])
                                S.op("dve", lambda e: e.scalar_tensor_tensor(y_o[:], x_o[:], sm[:, 2:3], fgb[:], ALU.mult, ALU.mult),
                                     reads=[ok2, "smq2"], writes=[yk])
                                S.dma("sp", y_out[tsl, :], y_o[:], reads=[yk], writes=[("y", tb)])
                    S.barrier()
        S.barrier()
    return nc


def _rope_tables(n_tokens, dim):
    rows = n_tokens // 64
    row = np.repeat(np.arange(rows, dtype=np.float32), 64)
    col = np.tile(np.arange(64, dtype=np.float32), rows)
    quarter = dim // 4
    inv = (np.float32(10000.0) ** (-np.arange(quarter, dtype=np.float32) / np.float32(quarter))).astype(np.float32)
    ang = np.concatenate([row[:, None] * inv, col[:, None] * inv], axis=-1).astype(np.float32)
    return np.cos(ang).astype(np.float32), np.sin(ang).astype(np.float32)


_NC_CACHE = {}


def prepare_inputs(x_prompt, x_sample, cache_diff_k, cache_diff_v, cache_gqa_k, cache_gqa_v, c, c_ctx,
                   ada_w, ada_b, norm1_g, norm2_g, w_in, sgu_norm_g, sgu_w, sgu_b,
                   diff_lq1, diff_lk1, diff_lq2, diff_lk2, diff_subln_g, gqa_qnorm_g, gqa_knorm_g, w_out,
                   peer_wq, peer_subkeys, peer_u, peer_v, final_g, cores=range(8)):
    f = lambda a: np.ascontiguousarray(np.asarray(a, dtype=np.float32))
    x_prompt = f(x_prompt); x_sample = f(x_sample)
    peer_u = f(peer_u); peer_v = f(peer_v)
    utl = np.ascontiguousarray(peer_u.reshape(L, 128, 128, 8, 128).transpose(0, 1, 4, 3, 2))
    vtl = peer_v.reshape(L, 128 * 128, D)
    sk = f(peer_subkeys)
    skbd = np.zeros((L, 128, 256), np.float32)
    for x in range(2):
        skbd[:, x * 64:(x + 1) * 64, x * 128:(x + 1) * 128] = sk[:, x].transpose(0, 2, 1)
    shared = {
        "ada_w": f(ada_w), "ada_b": f(ada_b), "norm1_g": f(norm1_g), "norm2_g": f(norm2_g),
        "w_in": f(w_in), "sgu_norm_g": f(sgu_norm_g),
        "sgu_wT": np.ascontiguousarray(f(sgu_w).transpose(0, 1, 3, 2)),
        "sgu_bT": np.ascontiguousarray(f(sgu_b).transpose(0, 2, 1)),
        "lq1": f(diff_lq1), "lk1": f(diff_lk1), "lq2": f(diff_lq2), "lk2": f(diff_lk2),
        "subln_g": f(diff_subln_g), "qn_g": f(gqa_qnorm_g), "kn_g": f(gqa_knorm_g),
        "w_out": f(w_out), "wq": f(peer_wq), "skbd": skbd, "ut": utl, "vt": vtl,
        "final_g": f(final_g).reshape(1, D),
    }
    cd, sd = _rope_tables(NTOK, 32)
    cg, sg = _rope_tables(NTOK, 64)
    mp = np.zeros((NKB, 8), np.float32)
    for kb in range(NKB):
        for q in range(8):
            if kb >= NB or kb // 2 != q:
                mp[kb, q] = NEG
    mp = np.broadcast_to(mp.reshape(1, -1), (128, NKB * 8)).copy()
    ms = np.zeros((128, NKB * 8), np.float32)
    cdk = f(cache_diff_k).reshape(4, L, PAST, 256); cdv = f(cache_diff_v).reshape(4, L, PAST, 256)
    cgk = f(cache_gqa_k).reshape(4, L, PAST, 128); cgv = f(cache_gqa_v).reshape(4, L, PAST, 128)
    z256 = np.zeros((L, PAST, 256), np.float32); z128 = np.zeros((L, PAST, 128), np.float32)
    ones16 = np.ones((NTOK, 16), np.float32); zer16 = np.zeros((NTOK, 16), np.float32)
    ones32 = np.ones((NTOK, 32), np.float32); zer32 = np.zeros((NTOK, 32), np.float32)
    c = f(c); c_ctx = f(c_ctx)
    in_maps = []
    for core in cores:
        m = dict(shared)
        if core < 4:
            m["x"] = x_prompt[8 * core:8 * core + 8].reshape(NTOK, D)
            cv = c_ctx
            m.update(cdk=z256, cdv=z256, cgk=z128, cgv=z128, maskb=mp,
                     cosd=ones16, sind=zer16, cosg=ones32, sing=zer32)
        else:
            b = core - 4
            m["x"] = x_sample[b]
            cv = c[b]
            m.update(cdk=cdk[b], cdv=cdv[b], cgk=cgk[b], cgv=cgv[b], maskb=ms,
                     cosd=cd, sind=sd, cosg=cg, sing=sg)
        m["cvec"] = np.ascontiguousarray(cv.reshape(8, 128).T)
        in_maps.append(m)
    return in_maps


def kernel(**inputs):
    in_maps = prepare_inputs(**inputs)
    if "nc" not in _NC_CACHE:
        _NC_CACHE["nc"] = build_program()
    nc = _NC_CACHE["nc"]
    res = run_bass_kernel_spmd(nc, in_maps, core_ids=list(range(8)))
    r = res.results
    y_prompt = np.concatenate([r[i]["y"].reshape(8, 256, D) for i in range(4)], axis=0)
    y_sample = np.stack([r[4 + i]["y"] for i in range(4)], axis=0)

    def gather(name, tail):
        parts = []
        for i in range(4):
            a = r[i][name]
            a = a.reshape(L, 8, 256, -1).transpose(1, 0, 2, 3)
            parts.append(a)
        return np.ascontiguousarray(np.concatenate(parts, axis=0).reshape((32, L, 256) + tail))

    return (np.ascontiguousarray(y_prompt.astype(np.float32)), np.ascontiguousarray(y_sample.astype(np.float32)),
            gather("ndk", (4, 2, 32)), gather("ndv", (4, 64)), gather("ngk", (2, 64)), gather("ngv", (2, 64)))
```

```python
import math
import os
from contextlib import ExitStack

import numpy as np
import concourse.bass as bass
import concourse.mybir as mybir
from concourse.bass_utils import run_bass_kernel_spmd

F32 = mybir.dt.float32
BF16 = mybir.dt.bfloat16
U32 = mybir.dt.uint32
AF = mybir.ActivationFunctionType
ALU = mybir.AluOpType
AX = mybir.AxisListType

D = 1024
L = 2
NTOK = 2048
NB = 16
PAST = 512
NKB = 20
EPS = 1e-6
DIFF_SCALE = 32 ** -0.5
GQA_SCALE = 64 ** -0.5
EPOCH = 12000
TB = 256
NEG = -30000.0


class Sched:
    def __init__(self, nc, n_dma_sems=48):
        self.nc = nc
        self.eng = {"pe": nc.tensor, "act": nc.scalar, "dve": nc.vector,
                    "pool": nc.gpsimd, "sp": nc.sync}
        self.cnt = {e: 0 for e in self.eng}
        self.sems = {e: [] for e in self.eng}
        self.waited = {e: {} for e in self.eng}
        self.n_hw = n_dma_sems
        self.dma_sems = [nc.alloc_semaphore(name=f"dq{i}") for i in range(n_dma_sems)]
        self.dma_val = [0] * n_dma_sems
        self.dma_free = list(range(n_dma_sems))
        self.dma_out = []
        self.dma_waited = {e: {} for e in self.eng}
        self.last_w = {}
        self.readers = {}

    def _sem_for(self, e, n):
        ep = (n - 1) // EPOCH
        while len(self.sems[e]) <= ep:
            self.sems[e].append(self.nc.alloc_semaphore(name=f"s_{e}_{len(self.sems[e])}"))
        return self.sems[e][ep], (n - 1) % EPOCH + 1

    def _wait(self, e, tok):
        eng = self.eng[e]
        if tok[0] == "dma":
            _, idx, val = tok
            if self.dma_waited[e].get(idx, 0) >= val:
                return
            eng.wait_ge(self.dma_sems[idx], val)
            self.dma_waited[e][idx] = val
        else:
            src, n = tok
            if src == e and e == "pe":
                return
            if self.waited[e].get(src, 0) >= n:
                return
            if src == e:
                n = self.cnt[e]
            sem, v = self._sem_for(src, n)
            eng.wait_ge(sem, v)
            self.waited[e][src] = n

    def _deps(self, reads, writes, e=None):
        deps = []
        for k in reads:
            if k in self.last_w:
                deps.append((k, self.last_w[k]))
        for k in writes:
            if k in self.last_w:
                deps.append((k, self.last_w[k]))
            deps.extend((k, t) for t in self.readers.get(k, ()))
        out = []
        for k, t in deps:
            if isinstance(k, tuple) and k and k[0] == "ps" and t[0] == e:
                continue
            out.append(t)
        return out

    def _record(self, tok, reads, writes):
        for k in reads:
            self.readers.setdefault(k, []).append(tok)
        for k in writes:
            self.last_w[k] = tok
            self.readers[k] = []

    def op(self, e, fn, reads=(), writes=(), inc=True):
        for d in self._deps(reads, writes, e):
            self._wait(e, d)
        inst = fn(self.eng[e])
        if not inc:
            tok = (e, self.cnt[e] + 1)
            self._record(tok, reads, writes)
            return tok
        self.cnt[e] += 1
        n = self.cnt[e]
        sem, _ = self._sem_for(e, n)
        inst.then_inc(sem, 1)
        self._record((e, n), reads, writes)
        return (e, n)

    def dma(self, e, out, in_, reads=(), writes=(), **kw):
        for d in self._deps(reads, writes):
            self._wait(e, d)
        if e == "pool":
            idx = len(self.dma_sems)
            self.dma_sems.append(self.nc.alloc_semaphore(name=f"sw{idx}"))
            self.dma_val.append(0)
        else:
            while not self.dma_free:
                idx, val = self.dma_out.pop(0)
                self._wait(e, ("dma", idx, val))
                if idx < self.n_hw:
                    self.dma_free.append(idx)
            idx = self.dma_free.pop(0)
        self.dma_val[idx] += 16
        val = self.dma_val[idx]
        self.eng[e].dma_start(out=out, in_=in_, **kw).then_inc(self.dma_sems[idx], 16)
        self.dma_out.append((idx, val))
        tok = ("dma", idx, val)
        self._record(tok, reads, writes)
        return tok

    def bg_dma(self, e, out, in_, **kw):
        self.n_bg = getattr(self, "n_bg", 0) + 1
        sem = self.nc.alloc_semaphore(name=f"bg{self.n_bg}")
        self.eng[e].dma_start(out=out, in_=in_, **kw).then_inc(sem, 16)
        return (sem, 16)

    def wait_bg(self, e, tok):
        self.eng[e].wait_ge(tok[0], tok[1])

    def barrier(self):
        toks = [(e, self.cnt[e]) for e in self.eng if self.cnt[e] > 0]
        dmas = list(self.dma_out)
        for e in self.eng:
            for t in toks:
                if not (t[0] == e and e in ("pe", "sp")):
                    self._wait(e, t)
            for idx, val in dmas:
                self._wait(e, ("dma", idx, val))
        self.dma_out = []
        self.dma_free = list(range(self.n_hw))
        self.last_w = {}
        self.readers = {}


def build_program(n_layers=L, stop_after=None, debug=False, p2_passes=None, p2_chunks=128):
    nc = bass.Bass("TRN2", target_bir_lowering=False)

    def din(name, shape, dt=F32):
        return nc.dram_tensor(name, list(shape), dt, kind="ExternalInput").ap()

    def dout(name, shape, dt=F32):
        return nc.dram_tensor(name, list(shape), dt, kind="ExternalOutput").ap()

    x_in = din("x", [NTOK, D])
    cvec = din("cvec", [128, 8])
    cdk = din("cdk", [L, PAST, 256])
    cdv = din("cdv", [L, PAST, 256])
    cgk = din("cgk", [L, PAST, 128])
    cgv = din("cgv", [L, PAST, 128])
    maskb_d = din("maskb", [128, NKB * 8])
    cosd_d = din("cosd", [NTOK, 16])
    sind_d = din("sind", [NTOK, 16])
    cosg_d = din("cosg", [NTOK, 32])
    sing_d = din("sing", [NTOK, 32])
    ada_w = din("ada_w", [L, D, 6 * D])
    ada_b = din("ada_b", [L, 6 * D])
    norm1_g = din("norm1_g", [L, D])
    norm2_g = din("norm2_g", [L, D])
    w_in = din("w_in", [L, D, 2048])
    sgu_norm_g = din("sgu_norm_g", [L, 256])
    sgu_wT = din("sgu_wT", [L, 4, 128, 128])
    sgu_bT = din("sgu_bT", [L, 128, 4])
    lq1 = din("lq1", [L, 32]); lk1 = din("lk1", [L, 32])
    lq2 = din("lq2", [L, 32]); lk2 = din("lk2", [L, 32])
    subln_g = din("subln_g", [L, 64])
    qn_g = din("qn_g", [L, 64]); kn_g = din("kn_g", [L, 64])
    w_out = din("w_out", [L, D, D])
    wq = din("wq", [L, D, D])
    skbd = din("skbd", [L, 128, 256])
    ut = din("ut", [L, 128, 128, 8, 128])
    vt = din("vt", [L, 128 * 128, D])
    final_g = din("final_g", [1, D])

    y_out = dout("y", [NTOK, D])
    ndk = dout("ndk", [L, NTOK, 256])
    ndv = dout("ndv", [L, NTOK, 256])
    ngk = dout("ngk", [L, NTOK, 128])
    ngv = dout("ngv", [L, NTOK, 128])

    xs = nc.dram_tensor("xs", [NTOK, D], F32, kind=("ExternalOutput" if debug else "Internal")).ap()

    h2s = nc.dram_tensor("h2s", [NB, 128, 8 * 128], BF16, kind="Internal").ap()
    ijg = nc.dram_tensor("ijg", [3, 128, NTOK], F32, kind="Internal").ap()
    utb_l = [nc.dram_tensor(f"utb{i}", [128, 128, 8 * 128], BF16, kind="Internal").ap() for i in range(L)]
    vtb_l = [nc.dram_tensor(f"vtb{i}", [128, 128, D], BF16, kind="Internal").ap() for i in range(L)]

    with ExitStack() as G:
        def sbg(name, shape, dt=F32):
            return G.enter_context(nc.sbuf_tensor(name, list(shape), dt))

        ps = G.enter_context(nc.psum_tensor("ps", [128, 8, 512], F32))

        def psb(b):
            return ps[:, b, :].bitcast(BF16)

        def pb(b):
            return ("ps", b)

        S = Sched(nc)

        iota_f = sbg("iota_f", [128, 128])
        pid = sbg("pid", [128, 1])
        idb = sbg("idb", [128, 128], BF16)
        idf = sbg("idf", [128, 128])
        iota16 = sbg("iota16", [128, 16])
        iota_b = sbg("iota_b", [128, 128], BF16)
        maskb = sbg("maskb_t", [128, NKB * 8])
        mhalf = sbg("mhalf", [128, 16])
        c4u = sbg("c4u", [128, 1], U32)
        c15u = sbg("c15u", [128, 1], U32)
        g2t = sbg("g2t", [128, D])
        fgb = sbg("fgb", [128, D])
        neglam = sbg("neglam", [128, 1])
        S.op("pool", lambda e: e.iota(iota_f[:], [[1, 128]], base=0, channel_multiplier=0,
                                      allow_small_or_imprecise_dtypes=True), writes=["iota_f"])
        S.op("pool", lambda e: e.iota(pid[:], [[0, 1]], base=0, channel_multiplier=1,
                                      allow_small_or_imprecise_dtypes=True), writes=["pid"])
        S.op("pool", lambda e: e.iota(iota16[:], [[1, 16]], base=0, channel_multiplier=0,
                                      allow_small_or_imprecise_dtypes=True), writes=["iota16"])
        S.op("dve", lambda e: e.tensor_scalar(idb[:], iota_f[:], pid[:, 0:1], None, ALU.is_equal),
             reads=["iota_f", "pid"], writes=["idb"])
        S.op("dve", lambda e: e.tensor_scalar(idf[:], iota_f[:], pid[:, 0:1], None, ALU.is_equal),
             reads=["iota_f", "pid"], writes=["idf"])
        S.op("dve", lambda e: e.memset(mhalf[:], -0.5), writes=["mhalf"])
        S.op("dve", lambda e: e.tensor_copy(iota_b[:], iota_f[:]), reads=["iota_f"], writes=["iota_b"])
        S.op("pool", lambda e: e.iota(c4u[:], [[0, 1]], base=4, channel_multiplier=0), writes=["c4u"])
        S.op("pool", lambda e: e.iota(c15u[:], [[0, 1]], base=15, channel_multiplier=0), writes=["c15u"])
        S.dma("sp", maskb[:], maskb_d, writes=["maskb"])
        S.dma("sp", fgb[:], final_g.to_broadcast([128, D]), writes=["fgb"])
        S.barrier()

        def rstd_from_ssq(ssq_ap, out_ap, n, width, key_in, key_out, tmpkey, tmp_ap):
            S.op("dve", lambda e: e.tensor_scalar(tmp_ap, ssq_ap, 1.0 / width, EPS, ALU.mult, ALU.add),
                 reads=[key_in], writes=[tmpkey])
            S.op("pool", lambda e: e.tensor_tensor(out_ap, tmp_ap, mhalf[:, 0:n], ALU.pow),
                 reads=[tmpkey, "mhalf"], writes=[key_out])

        for l in range(n_layers):
            lam_init = 0.8 - 0.6 * math.exp(-0.3 * l)
            PM = ExitStack()
            G.callback(PM.close)
            mod = PM.enter_context(nc.sbuf_tensor(f"mod_{l}", [128, 6, D], F32))
            with ExitStack() as P:
                def sb(name, shape, dt=F32):
                    return P.enter_context(nc.sbuf_tensor(f"{name}_{l}", list(shape), dt))
                cv = sb("cv", [128, 8]); scv = sb("scv", [128, 8])
                crep = sb("crep", [128, 8, 128])
                awt = [sb(f"awt{i}", [128, 8, 512]) for i in range(3)]
                g1b = sb("g1b", [128, D]); g2b = sb("g2b", [128, D])
                lqk = sb("lqk", [128, 4, 32]); lss = sb("lss", [128, 2]); lex = sb("lex", [128, 2])
                ljunk = sb("ljunk", [128, 32])
                S.dma("sp", cv[:], cvec, writes=["cv"])
                S.dma("sp", mod[:].rearrange("p a b -> p (a b)"),
                      ada_b[l:l + 1, :].to_broadcast([128, 6 * D]), writes=["mod"])
                S.dma("sp", g1b[:], norm1_g[l:l + 1, :].to_broadcast([128, D]), writes=["g1b"])
                S.dma("sp", g2b[:], norm2_g[l:l + 1, :].to_broadcast([128, D]), writes=["g2b"])
                for i, t in enumerate((lq1, lk1, lq2, lk2)):
                    S.dma("sp", lqk[:, i, :], t[l:l + 1, :].to_broadcast([128, 32]), writes=[("lqk", i)])
                S.op("act", lambda e: e.activation(scv[:], cv[:], AF.Silu), reads=["cv"], writes=["scv"])
                S.op("dve", lambda e: e.tensor_copy(crep[:], scv[:].unsqueeze(2).to_broadcast([128, 8, 128])),
                     reads=["scv"], writes=["crep"])
                awv = ada_w[l].rearrange("(dc p) n -> p dc n", p=128)
                modf = mod[:].rearrange("p a b -> p (a b)")
                for c in range(12):
                    a = awt[c % 3]
                    if c == 0:
                        for c0 in range(3):
                            S.dma("sp", awt[c0][:], awv[:, :, c0 * 512:(c0 + 1) * 512], writes=[("awt", c0)])
                    bk = c % 2
                    for dc in range(8):
                        S.op("pe", lambda e: e.matmul(ps[:, bk, :], crep[:, dc, :], a[:, dc, :],
                                                      start=(dc == 0), stop=(dc == 7)),
                             reads=["crep", ("awt", c % 3)], writes=[pb(bk)])
                    if c + 3 < 12:
                        S.dma("sp", a[:], awv[:, :, (c + 3) * 512:(c + 4) * 512], writes=[("awt", c % 3)])
                    S.op("dve", lambda e: e.tensor_tensor(modf[:, c * 512:(c + 1) * 512], ps[:, bk, :],
                                                          modf[:, c * 512:(c + 1) * 512], ALU.add),
                         reads=[], writes=[pb(bk), "mod"])
                S.op("dve", lambda e: e.scalar_tensor_tensor(mod[:, 1, :], mod[:, 1, :], 1.0, g1b[:], ALU.add, ALU.mult),
                     reads=["g1b"], writes=["mod"])
                S.op("dve", lambda e: e.scalar_tensor_tensor(mod[:, 4, :], mod[:, 4, :], 1.0, g2b[:], ALU.add, ALU.mult),
                     reads=["g2b"], writes=["mod"])
                for j in range(2):
                    S.op("dve", lambda e: e.scalar_tensor_tensor(ljunk[:], lqk[:, 2 * j, :], 1.0, lqk[:, 2 * j + 1, :],
                                                                 ALU.mult, ALU.mult, accum_out=lss[:, j:j + 1]),
                         reads=[("lqk", 2 * j), ("lqk", 2 * j + 1)], writes=["ljunk", ("lss", j)])
                S.op("act", lambda e: e.activation(lex[:], lss[:], AF.Exp), reads=[("lss", 0), ("lss", 1)], writes=["lex"])
                S.op("dve", lambda e: e.scalar_tensor_tensor(neglam[:], lex[:, 1:2], -lam_init, lex[:, 0:1],
                                                             ALU.add, ALU.subtract),
                     reads=["lex"], writes=["neglam"])
                S.barrier()

            x_src = x_in if l == 0 else xs
            with ExitStack() as PA:
                def sba(name, shape, dt=F32):
                    return PA.enter_context(nc.sbuf_tensor(f"{name}_{l}", list(shape), dt))
                KTd = sba("KTd", [128, 2, NKB * 128], BF16)
                KTg = sba("KTg", [128, NKB * 128], BF16)
                QTd = sba("QTd", [128, 2, NTOK], BF16)
                QTg = sba("QTg", [128, 4, NTOK], BF16)
                Vd = sba("Vd", [128, NKB, 4, 65], BF16)
                Vg = sba("Vg", [128, NKB, 2, 65], BF16)
                mixA = sba("mixA", [128, NB, 256], BF16)
                S.op("pool", lambda e: e.memset(Vd[:].rearrange("p a b c -> p (a b c)"), 1.0), writes=["Vd"])
                S.op("pool", lambda e: e.memset(Vg[:].rearrange("p a b c -> p (a b c)"), 1.0), writes=["Vg"])
                S.barrier()

                with ExitStack() as P:
                    def sb(name, shape, dt=F32):
                        return P.enter_context(nc.sbuf_tensor(f"{name}_{l}", list(shape), dt))
                    wib = sb("wib", [128, 8, 2048], BF16)
                    swT = sb("swT", [128, 4, 128], BF16)
                    sgub = sb("sgub", [128, 4])
                    sgng = sb("sgng", [128, 256])
                    qkg = sb("qkg", [128, 10, 64])
                    rope = sb("rope", [128, NB, 96])
                    xt = [sb(f"xt{i}", [128, D]) for i in range(2)]
                    hfP = [sb(f"hf{i}", [128, D]) for i in range(2)]
                    hbP = [sb(f"hb{i}", [128, D], BF16) for i in range(2)]
                    hTP = [sb(f"hT{i}", [128, 8, 128], BF16) for i in range(2)]
                    smP = [sb(f"sm{i}", [128, 64]) for i in range(2)]
                    stP = [sb(f"st{i}", [128, 1536]) for i in range(2)]
                    auP = [sb(f"au{i}", [128, 256]) for i in range(2)]
                    avP = [sb(f"av{i}", [128, 256]) for i in range(2)]
                    avn = sb("avn", [128, 256], BF16)
                    tmpa = sb("tmpa", [128, 640])
                    gn = sb("gn", [128, 640])
                    rt = sb("rt", [128, 4, 320])
                    rq = sb("rq", [128, 4, 256])
                    qkr = sb("qkr", [128, 1152], BF16)
                    cst4 = sb("cst", [128, 4, 768], BF16)
                    wiv = w_in[l].rearrange("(dc p) n -> p dc n", p=128)
                    S.dma("pool", wib[:], wiv, writes=["wib"])
                    S.dma("pool", swT[:], sgu_wT[l].rearrange("g q p -> q g p"), writes=["swT"])
                    S.dma("sp", sgub[:], sgu_bT[l], writes=["sgub"])
                    S.dma("sp", sgng[:], sgu_norm_g[l:l + 1, :].to_broadcast([128, 256]), writes=["sgng"])
                    S.dma("sp", qkg[:, 0:8, :], qn_g[l:l + 1, :].unsqueeze(1).to_broadcast([128, 8, 64]), writes=["qkg0"])
                    S.dma("sp", qkg[:, 8:10, :], kn_g[l:l + 1, :].unsqueeze(1).to_broadcast([128, 2, 64]), writes=["qkg1"])
                    S.dma("sp", rope[:, :, 0:16], cosd_d.rearrange("(tb p) c -> p tb c", p=128), writes=["rope0"])
                    S.dma("sp", rope[:, :, 16:32], sind_d.rearrange("(tb p) c -> p tb c", p=128), writes=["rope1"])
                    S.dma("sp", rope[:, :, 32:64], cosg_d.rearrange("(tb p) c -> p tb c", p=128), writes=["rope2"])
                    S.dma("sp", rope[:, :, 64:96], sing_d.rearrange("(tb p) c -> p tb c", p=128), writes=["rope3"])
                    S.barrier()

                    def a1_load(tb):
                        S.dma("sp", xt[tb % 2][:], x_src[tb * 128:(tb + 1) * 128, :], reads=[("xs", tb)], writes=[("xt", tb % 2)])

                    def a1_fa(tb):
                        pz = tb % 2
                        hf, hb, hT, au, av, st, sm = hfP[pz], hbP[pz], hTP[pz], auP[pz], avP[pz], stP[pz], smP[pz]
                        x_t = xt[tb % 2]
                        xk = ("xt", tb % 2)
                        S.op("dve", lambda e: e.scalar_tensor_tensor(hf[:], x_t[:], 1.0, x_t[:], ALU.mult, ALU.mult,
                                                                     accum_out=sm[:, 0:1]),
                             reads=[xk], writes=[("hf", pz), ("sm0", pz)])
                        rstd_from_ssq(sm[:, 0:1], sm[:, 2:3], 1, D, ("sm0", pz), ("sm2", pz), ("sm1", pz), sm[:, 1:2])
                        S.op("dve", lambda e: e.scalar_tensor_tensor(hf[:], x_t[:], sm[:, 2:3], mod[:, 1, :], ALU.mult, ALU.mult),
                             reads=[xk, ("sm2", pz)], writes=[("hf", pz)])
                        S.op("dve", lambda e: e.tensor_tensor(hb[:], hf[:], mod[:, 0, :], ALU.add),
                             reads=[("hf", pz)], writes=[("hb", pz)])
                        if tb + 2 < NB:
                            a1_load(tb + 2)
                        for dc in range(8):
                            S.op("pe", lambda e: e.transpose(psb(0)[:, dc * 128:(dc + 1) * 128], hb[:, dc * 128:(dc + 1) * 128], idb[:]),
                                 reads=[("hb", pz)], writes=[pb(0)])
                        S.op("act", lambda e: e.copy(hT[:].rearrange("p a b -> p (a b)"), psb(0)[:, :]),
                             reads=[], writes=[pb(0), ("hT", pz)])
                    def a1_fb(tb):
                        pz = tb % 2
                        hf, hb, hT, au, av, st, sm = hfP[pz], hbP[pz], hTP[pz], auP[pz], avP[pz], stP[pz], smP[pz]
                        for cc in range(4):
                            for dc in range(8):
                                S.op("pe", lambda e: e.matmul(ps[:, 1 + cc, :], hT[:, dc, :], wib[:, dc, cc * 512:(cc + 1) * 512],
                                                              start=(dc == 0), stop=(dc == 7)),
                                     reads=[("hT", pz)], writes=[pb(1 + cc)], inc=(dc == 7))
                        S.op("act", lambda e: e.activation(au[:], ps[:, 1, 0:256], AF.Gelu_apprx_tanh), writes=[pb(1), ("au", pz)])
                        S.op("act", lambda e: e.activation(av[:], ps[:, 1, 256:512], AF.Gelu_apprx_tanh), writes=[pb(1), ("av", pz)])
                        S.op("act", lambda e: e.copy(st[:, 0:512], ps[:, 2, :]), writes=[pb(2), ("st0", pz)])
                        S.op("act", lambda e: e.copy(st[:, 512:1024], ps[:, 3, :]), writes=[pb(3), ("st1", pz)])
                        S.op("act", lambda e: e.copy(st[:, 1024:1536], ps[:, 4, :]), writes=[pb(4), ("st2", pz)])
                    def a1_b1(tb):
                        pz = tb % 2
                        hf, hb, hT, au, av, st, sm = hfP[pz], hbP[pz], hTP[pz], auP[pz], avP[pz], stP[pz], smP[pz]
                        S.op("dve", lambda e: e.scalar_tensor_tensor(tmpa[:, 0:256], av[:], 1.0, av[:], ALU.mult, ALU.mult,
                                                                     accum_out=sm[:, 4:5]),
                             reads=[("av", pz)], writes=["tmpa", "sm4"])
                        rstd_from_ssq(sm[:, 4:5], sm[:, 6:7], 1, 256, "sm4", "sm6", "sm5", sm[:, 5:6])
                        S.op("dve", lambda e: e.scalar_tensor_tensor(avn[:], av[:], sm[:, 6:7], sgng[:], ALU.mult, ALU.mult),
                             reads=[("av", pz), "sm6"], writes=["avn"])
                        for g in range(4):
                            S.op("pe", lambda e: e.matmul(ps[:, 5, g * 64:(g + 1) * 64], swT[:, g, :], avn[:, g * 64:(g + 1) * 64],
                                                          start=True, stop=True, skip_group_check=True),
                                 reads=["avn"], writes=[pb(5)])
                        S.op("dve", lambda e: e.tensor_tensor(tmpa[:, 0:256].rearrange("p (g c) -> p g c", g=4),
                                                              ps[:, 5, 0:256].rearrange("p (g c) -> p g c", g=4),
                                                              sgub[:].unsqueeze(2).to_broadcast([128, 4, 64]), ALU.add),
                             writes=[pb(5), "tmpa"])
                        S.op("dve", lambda e: e.tensor_tensor(mixA[:, tb, :], tmpa[:, 0:256], au[:], ALU.mult),
                             reads=["tmpa", ("au", pz)], writes=[("mixA", tb)])
                    def a1_b2(tb):
                        pz = tb % 2
                        hf, hb, hT, au, av, st, sm = hfP[pz], hbP[pz], hTP[pz], auP[pz], avP[pz], stP[pz], smP[pz]
                        S.dma("sp", ndk[l, tb * 128:(tb + 1) * 128, :], st[:, 256:512], reads=[("st0", pz)], writes=[("ndk", tb)])
                        S.dma("sp", ndv[l, tb * 128:(tb + 1) * 128, :], st[:, 512:768], reads=[("st1", pz)], writes=[("ndv", tb)])
                        S.dma("sp", ngv[l, tb * 128:(tb + 1) * 128, :], st[:, 1408:1536], reads=[("st2", pz)], writes=[("ngv", tb)])
                        S.op("pool", lambda e: e.tensor_copy(Vd[:, tb, :, 0:64], st[:, 512:768].rearrange("p (h d) -> p h d", h=4)),
                             reads=[("st1", pz)], writes=[("Vd", tb)])
                        S.op("pool", lambda e: e.tensor_copy(Vg[:, tb, :, 0:64], st[:, 1408:1536].rearrange("p (h d) -> p h d", h=2)),
                             reads=[("st2", pz)], writes=[("Vg", tb)])
                        gq = st[:, 768:1408]
                        S.op("dve", lambda e: e.tensor_tensor(tmpa[:], gq, gq, ALU.mult), reads=[("st1", pz), ("st2", pz)], writes=["tmpa"])
                        S.op("dve", lambda e: e.tensor_reduce(sm[:, 8:18], tmpa[:].rearrange("p (h d) -> p h d", h=10), AX.X, ALU.add),
                             reads=["tmpa"], writes=["sm8"])
                        rstd_from_ssq(sm[:, 8:18], sm[:, 28:38], 10, 64, "sm8", "sm28", "sm18", sm[:, 18:28])
                        S.op("dve", lambda e: e.tensor_tensor(gn[:].rearrange("p (h d) -> p h d", h=10),
                                                              gq.rearrange("p (h d) -> p h d", h=10),
                                                              sm[:, 28:38].unsqueeze(2).to_broadcast([128, 10, 64]), ALU.mult),
                             reads=[("st1", pz), ("st2", pz), "sm28"], writes=["gn"])
                        S.op("dve", lambda e: e.tensor_tensor(gn[:], gn[:], qkg[:].rearrange("p h d -> p (h d)"), ALU.mult),
                             reads=[], writes=["gn"])
                        S.dma("sp", ngk[l, tb * 128:(tb + 1) * 128, :], gn[:, 512:640], reads=["gn"], writes=[("ngk", tb)])
                        dqk = st[:, 0:512].rearrange("p (g d) -> p g d", g=16)
                        cd = rope[:, tb, 0:16].unsqueeze(1).to_broadcast([128, 16, 16])
                        sd = rope[:, tb, 16:32].unsqueeze(1).to_broadcast([128, 16, 16])
                        r0 = rt[:, 0, 0:256].rearrange("p (g d) -> p g d", g=16)
                        r1 = rt[:, 1, 0:256].rearrange("p (g d) -> p g d", g=16)
                        r2 = rt[:, 2, 0:256].rearrange("p (g d) -> p g d", g=16)
                        r3 = rt[:, 3, 0:256].rearrange("p (g d) -> p g d", g=16)
                        od = qkr[:, 0:512].rearrange("p (g d) -> p g d", g=16)
                        S.op("dve", lambda e: e.tensor_tensor(r0, dqk[:, :, 0:16], cd, ALU.mult), reads=[("st0", pz)], writes=["rt0"])
                        S.op("pool", lambda e: e.tensor_tensor(r1, dqk[:, :, 16:32], sd, ALU.mult), reads=[("st0", pz)], writes=["rt1"])
                        S.op("dve", lambda e: e.tensor_tensor(r2, dqk[:, :, 16:32], cd, ALU.mult), reads=[("st0", pz)], writes=["rt2"])
                        S.op("pool", lambda e: e.tensor_tensor(r3, dqk[:, :, 0:16], sd, ALU.mult), reads=[("st0", pz)], writes=["rt3"])
                        S.op("dve", lambda e: e.tensor_tensor(od[:, :, 0:16], r0, r1, ALU.subtract), reads=["rt0", "rt1"], writes=["qkr0"])
                        S.op("dve", lambda e: e.tensor_tensor(od[:, :, 16:32], r2, r3, ALU.add), reads=["rt2", "rt3"], writes=["qkr1"])
                        gnq = gn[:, 0:512].rearrange("p (u t d) -> p u t d", u=2, t=4)
                        oq = qkr[:, 512:1024].rearrange("p (t u d) -> p u t d", t=4, u=2)
                        cg4 = rope[:, tb, 32:64].unsqueeze(1).unsqueeze(1).to_broadcast([128, 2, 4, 32])
                        sg4 = rope[:, tb, 64:96].unsqueeze(1).unsqueeze(1).to_broadcast([128, 2, 4, 32])
                        q0 = rq[:, 0, 0:256].rearrange("p (u t d) -> p u t d", u=2, t=4)
                        q1 = rq[:, 1, 0:256].rearrange("p (u t d) -> p u t d", u=2, t=4)
                        q2 = rq[:, 2, 0:256].rearrange("p (u t d) -> p u t d", u=2, t=4)
                        q3 = rq[:, 3, 0:256].rearrange("p (u t d) -> p u t d", u=2, t=4)
                        S.op("dve", lambda e: e.tensor_tensor(q0, gnq[:, :, :, 0:32], cg4, ALU.mult), reads=["gn"], writes=["rq0"])
                        S.op("pool", lambda e: e.tensor_tensor(q1, gnq[:, :, :, 32:64], sg4, ALU.mult), reads=["gn"], writes=["rq1"])
                        S.op("dve", lambda e: e.tensor_tensor(q2, gnq[:, :, :, 32:64], cg4, ALU.mult), reads=["gn"], writes=["rq2"])
                        S.op("pool", lambda e: e.tensor_tensor(q3, gnq[:, :, :, 0:32], sg4, ALU.mult), reads=["gn"], writes=["rq3"])
                        S.op("dve", lambda e: e.tensor_tensor(oq[:, :, :, 0:32], q0, q1, ALU.subtract), reads=["rq0", "rq1"], writes=["qkr2"])
                        S.op("dve", lambda e: e.tensor_tensor(oq[:, :, :, 32:64], q2, q3, ALU.add), reads=["rq2", "rq3"], writes=["qkr3"])
                        gnk = gn[:, 512:640].rearrange("p (h d) -> p h d", h=2)
                        ok_ = qkr[:, 1024:1152].rearrange("p (h d) -> p h d", h=2)
                        cg2 = rope[:, tb, 32:64].unsqueeze(1).to_broadcast([128, 2, 32])
                        sg2 = rope[:, tb, 64:96].unsqueeze(1).to_broadcast([128, 2, 32])
                        k0 = rt[:, 0, 256:320].rearrange("p (h d) -> p h d", h=2)
                        k1 = rt[:, 1, 256:320].rearrange("p (h d) -> p h d", h=2)
                        k2 = rt[:, 2, 256:320].rearrange("p (h d) -> p h d", h=2)
                        k3 = rt[:, 3, 256:320].rearrange("p (h d) -> p h d", h=2)
                        S.op("dve", lambda e: e.tensor_tensor(k0, gnk[:, :, 0:32], cg2, ALU.mult), reads=["gn"], writes=["rk0"])
                        S.op("pool", lambda e: e.tensor_tensor(k1, gnk[:, :, 32:64], sg2, ALU.mult), reads=["gn"], writes=["rk1"])
                        S.op("dve", lambda e: e.tensor_tensor(k2, gnk[:, :, 32:64], cg2, ALU.mult), reads=["gn"], writes=["rk2"])
                        S.op("pool", lambda e: e.tensor_tensor(k3, gnk[:, :, 0:32], sg2, ALU.mult), reads=["gn"], writes=["rk3"])
                        S.op("dve", lambda e: e.tensor_tensor(ok_[:, :, 0:32], k0, k1, ALU.subtract), reads=["rk0", "rk1"], writes=["qkr4"])
                        S.op("dve", lambda e: e.tensor_tensor(ok_[:, :, 32:64], k2, k3, ALU.add), reads=["rk2", "rk3"], writes=["qkr5"])
                        qkeys = ["qkr0", "qkr1", "qkr2", "qkr3", "qkr4", "qkr5"]
                        for j in range(8):
                            S.op("pe", lambda e: e.transpose(psb(6)[:, j * 128:(j + 1) * 128], qkr[:, j * 128:(j + 1) * 128], idb[:]),
                                 reads=qkeys, writes=[pb(6)])
                        S.op("pe", lambda e: e.transpose(psb(7)[:, 0:128], qkr[:, 1024:1152], idb[:]), reads=qkeys, writes=[pb(7)])
                        tsl = slice(tb * 128, (tb + 1) * 128)
                        S.op("act", lambda e: e.copy(QTd[:, :, tsl], psb(6)[:, 0:256].rearrange("p (a b) -> p a b", a=2)),
                             writes=[pb(6), ("QTd", tb)])
                        S.op("dve", lambda e: e.tensor_copy(KTd[:, :, tsl], psb(6)[:, 256:512].rearrange("p (a b) -> p a b", a=2)),
                             writes=[pb(6), ("KTd", tb)])
                        S.op("act", lambda e: e.copy(QTg[:, :, tsl], psb(6)[:, 512:1024].rearrange("p (a b) -> p a b", a=4)),
                             writes=[pb(6), ("QTg", tb)])
                        S.op("dve", lambda e: e.tensor_copy(KTg[:, tsl], psb(7)[:, 0:128]), writes=[pb(7), ("KTg", tb)])

                    a1_load(0)
                    a1_load(1)
                    a1_fa(0)
                    a1_fb(0)
                    for tb in range(NB):
                        if tb + 1 < NB:
                            a1_fa(tb + 1)
                        a1_b1(tb)
                        if tb + 1 < NB:
                            a1_fb(tb + 1)
                        a1_b2(tb)
                    S.dma("pool", cst4[:, :, 0:256], cdk[l].rearrange("(cb p) c -> p cb c", p=128), writes=["cst0"])
                    S.dma("pool", cst4[:, :, 256:384], cgk[l].rearrange("(cb p) c -> p cb c", p=128), writes=["cst1"])
                    S.dma("pool", cst4[:, :, 384:640], cdv[l].rearrange("(cb p) c -> p cb c", p=128), writes=["cst2"])
                    S.dma("pool", cst4[:, :, 640:768], cgv[l].rearrange("(cb p) c -> p cb c", p=128), writes=["cst3"])
                    for cb in range(4):
                        kb = NB + cb
                        cst = cst4[:, cb, :]
                        for j in range(3):
                            S.op("pe", lambda e: e.transpose(psb(6)[:, j * 128:(j + 1) * 128], cst[:, j * 128:(j + 1) * 128], idb[:]),
                                 reads=["cst0", "cst1"], writes=[pb(6)])
                        ksl = slice(kb * 128, (kb + 1) * 128)
                        S.op("act", lambda e: e.copy(KTd[:, :, ksl], psb(6)[:, 0:256].rearrange("p (a b) -> p a b", a=2)),
                             writes=[pb(6), ("KTd", kb)])
                        S.op("dve", lambda e: e.tensor_copy(KTg[:, ksl], psb(6)[:, 256:384]), writes=[pb(6), ("KTg", kb)])
                        S.op("pool", lambda e: e.tensor_copy(Vd[:, kb, :, 0:64], cst[:, 384:640].rearrange("p (h d) -> p h d", h=4)),
                             reads=["cst2"], writes=[("Vd", kb)])
                        S.op("pool", lambda e: e.tensor_copy(Vg[:, kb, :, 0:64], cst[:, 640:768].rearrange("p (h d) -> p h d", h=2)),
                             reads=["cst3"], writes=[("Vg", kb)])
                    S.barrier()
                if stop_after == ("A1", l):
                    S.barrier()
                    return nc

                with ExitStack() as P:
                    def sb(name, shape, dt=F32):
                        return P.enter_context(nc.sbuf_tensor(f"{name}_{l}", list(shape), dt))
                    wob = sb("wob", [128, 8, D], BF16)
                    slg = sb("slg", [128, 64])
                    PT = [sb(f"PT{i}", [128, 512], BF16) for i in range(4)]
                    mixBC = sb("mixBC", [128, 4, 768], BF16)
                    od2 = sb("od2", [128, 2, 4, 64])
                    rz = sb("rz", [128, 2, 4])
                    osb = sb("osb", [128, 4, 64]); osq = sb("osq", [128, 4, 64])
                    ss4 = sb("ss4", [128, 12])
                    mixT = sb("mixT", [128, 8, 128], BF16)
                    xt = [sb(f"xa{i}", [128, D]) for i in range(2)]
                    xo = [sb(f"xo{i}", [128, D]) for i in range(2)]
                    wov = w_out[l].rearrange("(dc p) n -> p dc n", p=128)
                    S.dma("pool", wob[:], wov, writes=["wob"])
                    S.dma("sp", slg[:], subln_g[l:l + 1, :].to_broadcast([128, 64]), writes=["slg"])
                    bg_toks = []
                    utf = ut[l].rearrange("i p a b -> (i p) (a b)")
                    utbf = utb_l[l].rearrange("i p n -> (i p) n")
                    vtbf = vtb_l[l].rearrange("i p n -> (i p) n")
                    for c8 in range(2):
                        rs8 = slice(c8 * 8192, (c8 + 1) * 8192)
                        bg_toks.append(S.bg_dma("pool", utbf[rs8, :], utf[rs8, :]))
                        bg_toks.append(S.bg_dma("pool", vtbf[rs8, :], vt[l, rs8, :]))
                    S.op("dve", lambda e: e.tensor_scalar(slg[:], slg[:], 1.0 - lam_init, None, ALU.mult), writes=["slg"])
                    S.barrier()
                    LA = 2
                    for qc in range(4):
                        qsl = slice(qc * 512, (qc + 1) * 512)

                        def map_cfg(m):
                            if m < 8:
                                j, r0, K = m // 4, 32 * (m % 4), 32
                                return dict(r0=r0, K=K, scale=DIFF_SCALE,
                                            kt=lambda kb: KTd[r0:r0 + K, j, kb * 128:(kb + 1) * 128],
                                            qt=QTd[r0:r0 + K, j, qsl],
                                            vv=lambda kb: Vd[:, kb, m // 2, :])
                            g = m - 8
                            j, r0, K = g % 4, 64 * (g // 4), 64
                            return dict(r0=r0, K=K, scale=GQA_SCALE,
                                        kt=lambda kb: KTg[r0:r0 + K, kb * 128:(kb + 1) * 128],
                                        qt=QTg[r0:r0 + K, j, qsl],
                                        vv=lambda kb: Vg[:, kb, g // 4, :])

                        cfgs = [map_cfg(m) for m in range(16)]
                        steps = [(m, kb) for m in range(16) for kb in range(NKB)]

                        def emit_st(n):
                            m, kb = steps[n]
                            c = cfgs[m]
                            sbk = n % 3
                            S.op("pe", lambda e: e.matmul(ps[:, sbk, :], c["kt"](kb), c["qt"], start=True, stop=True,
                                                          tile_position=(c["r0"], 0)),
                                 reads=[], writes=[pb(sbk)])
                            pt = PT[n % 4]
                            for hq in range(2):
                                col = kb * 8 + qc * 2 + hq
                                S.op("act", lambda e: e.activation(pt[:, hq * 256:(hq + 1) * 256], ps[:, sbk, hq * 256:(hq + 1) * 256],
                                                                   AF.Exp, bias=maskb[:, col:col + 1], scale=c["scale"]),
                                     reads=[], writes=[pb(sbk), ("PT", n % 4, hq)])

                        def emit_pv(n):
                            m, kb = steps[n]
                            c = cfgs[m]
                            pvb = 3 + (m % 2)
                            pt = PT[n % 4]
                            for qs in range(4):
                                S.op("pe", lambda e: e.matmul(ps[:, pvb, qs * 65:(qs + 1) * 65], pt[:, qs * 128:(qs + 1) * 128], c["vv"](kb),
                                                              start=(kb == 0 and qs == 0), stop=(kb == NKB - 1),
                                                              skip_group_check=True),
                                     reads=[("PT", n % 4, qs // 2)], writes=[pb(pvb)], inc=(qs == 3))
                            if kb != NKB - 1:
                                return
                            pv = ps[:, pvb, 0:260].rearrange("p (q c) -> p q c", q=4)
                            if m < 8:
                                xh = m % 2
                                S.op("dve", lambda e: e.reciprocal(rz[:, xh, :], pv[:, :, 64]), writes=[pb(pvb), ("rz", xh)])
                                S.op("dve", lambda e: e.tensor_tensor(od2[:, xh, :, :], pv[:, :, 0:64],
                                                                      rz[:, xh, :].unsqueeze(2).to_broadcast([128, 4, 64]), ALU.mult),
                                     reads=[("rz", xh)], writes=[pb(pvb), ("od2", xh)])
                                if xh == 1:
                                    h = m // 2
                                    S.op("dve", lambda e: e.scalar_tensor_tensor(osb[:].rearrange("p a b -> p (a b)"),
                                                                                 od2[:, 1, :, :].rearrange("p a b -> p (a b)"), neglam[:, 0:1],
                                                                                 od2[:, 0, :, :].rearrange("p a b -> p (a b)"), ALU.mult, ALU.add),
                                         reads=[("od2", 0), ("od2", 1)], writes=["osb"])
                                    S.op("dve", lambda e: e.tensor_tensor(osq[:], osb[:], osb[:], ALU.mult), reads=["osb"], writes=["osq"])
                                    S.op("dve", lambda e: e.tensor_reduce(ss4[:, 0:4], osq[:], AX.X, ALU.add), reads=["osq"], writes=["ss4a"])
                                    rstd_from_ssq(ss4[:, 0:4], ss4[:, 8:12], 4, 64, "ss4a", "ss4c", "ss4b", ss4[:, 4:8])
                                    S.op("dve", lambda e: e.tensor_tensor(osq[:], osb[:], ss4[:, 8:12].unsqueeze(2).to_broadcast([128, 4, 64]), ALU.mult),
                                         reads=["osb", "ss4c"], writes=["osq"])
                                    S.op("dve", lambda e: e.tensor_tensor(mixBC[:, :, h * 64:(h + 1) * 64], osq[:],
                                                                          slg[:].unsqueeze(1).to_broadcast([128, 4, 64]), ALU.mult),
                                         reads=["osq"], writes=[("mixBC", m)])
                            else:
                                g = m - 8
                                S.op("dve", lambda e: e.reciprocal(rz[:, 0, :], pv[:, :, 64]), writes=[pb(pvb), ("rz", 0)])
                                S.op("dve", lambda e: e.tensor_tensor(mixBC[:, :, 256 + g * 64:256 + (g + 1) * 64], pv[:, :, 0:64],
                                                                      rz[:, 0, :].unsqueeze(2).to_broadcast([128, 4, 64]), ALU.mult),
                                     reads=[("rz", 0)], writes=[pb(pvb), ("mixBC", m)])

                        for n in range(len(steps) + LA):
                            if n < len(steps):
                                emit_st(n)
                            if n - LA >= 0:
                                emit_pv(n - LA)
                        mkeys = [("mixBC", m) for m in range(16)]
                        for qs in range(4):
                            tb = qc * 4 + qs
                            x_t = xt[tb % 2]; xk = ("xa", tb % 2)
                            x_o = xo[tb % 2]; ok2 = ("xo", tb % 2)
                            S.dma("sp", x_t[:], x_src[tb * 128:(tb + 1) * 128, :], reads=[("xs", tb)], writes=[xk])
                            for dc in range(8):
                                src = mixA[:, tb, dc * 128:(dc + 1) * 128] if dc < 2 else mixBC[:, qs, (dc - 2) * 128:(dc - 1) * 128]
                                S.op("pe", lambda e: e.transpose(psb(5)[:, dc * 128:(dc + 1) * 128], src, idb[:]),
                                     reads=mkeys + [("mixA", tb)], writes=[pb(5)])
                            S.op("act", lambda e: e.copy(mixT[:].rearrange("p a b -> p (a b)"), psb(5)[:, :]), writes=[pb(5), "mixT"])
                            for cc in range(2):
                                for dc in range(8):
                                    S.op("pe", lambda e: e.matmul(ps[:, 6 + cc, :], mixT[:, dc, :], wob[:, dc, cc * 512:(cc + 1) * 512],
                                                                  start=(dc == 0), stop=(dc == 7)),
                                         reads=["mixT"], writes=[pb(6 + cc)], inc=(dc == 7))
                            for cc in range(2):
                                csl = slice(cc * 512, (cc + 1) * 512)
                                S.op("dve", lambda e: e.tensor_tensor(x_o[:, csl], ps[:, 6 + cc, :], mod[:, 2, csl], ALU.mult),
                                     writes=[pb(6 + cc), ok2])
                            S.op("pool", lambda e: e.tensor_tensor(x_o[:], x_o[:], x_t[:], ALU.add), reads=[xk], writes=[ok2])
                            S.dma("sp", xs[tb * 128:(tb + 1) * 128, :], x_o[:], reads=[ok2], writes=[("xs", tb)])
                    S.barrier()
            if stop_after == ("A3", l):
                S.barrier()
                return nc

            with ExitStack() as PP:
                def sbp(name, shape, dt=F32):
                    return PP.enter_context(nc.sbuf_tensor(f"{name}_{l}", list(shape), dt))
                with ExitStack() as P:
                    def sb(name, shape, dt=F32):
                        return P.enter_context(nc.sbuf_tensor(f"{name}_{l}", list(shape), dt))
                    wqb = sb("wqb", [128, 8, D], BF16)
                    skb = sb("skb", [128, 256], BF16)
                    xt = [sb(f"xp{i}", [128, D]) for i in range(2)]
                    hfP = [sb(f"hf2{i}", [128, D]) for i in range(2)]
                    hbP = [sb(f"hb2{i}", [128, D], BF16) for i in range(2)]
                    hTP = [sb(f"hT2{i}", [128, 8, 128], BF16) for i in range(2)]
                    qTP = [sb(f"qT{i}", [128, 8, 128], BF16) for i in range(2)]
                    smP = [sb(f"smp{i}", [128, 16]) for i in range(2)]
                    scP = [sb(f"sc{i}", [128, 2048]) for i in range(2)]
                    sc2 = sb("sc2", [128, 2048])
                    sv = sb("sv", [128, 16, 16]); si = sb("si", [128, 16, 16], U32); sif = sb("sif", [128, 16, 16])
                    cand = sb("cand", [128, 8, 256]); cand2 = sb("cand2", [128, 8, 256])
                    tsv = sb("tsv", [128, 8, 16]); tiu = sb("tiu", [128, 8, 16], U32)
                    au32 = sb("au32", [128, 128], U32); bu32 = sb("bu32", [128, 128], U32)
                    af_ = sb("af_", [128, 128]); bf_ = sb("bf_", [128, 128])
                    oh = sb("oh", [128, 2048]); IJ = sb("IJ", [128, 2, 128])
                    ex = sb("ex", [128, 8, 16]); zz = sb("zz", [128, 16]); gate = sb("gate", [128, 128])
                    ijs = sb("ijs", [128, 384])
                    wqv = wq[l].rearrange("(dc p) n -> p dc n", p=128)
                    S.dma("pool", wqb[:], wqv, writes=["wqb"])
                    S.dma("pool", skb[:], skbd[l], writes=["skb"])
                    S.barrier()
                    def p1_load(tb):
                        S.dma("sp", xt[tb % 2][:], xs[tb * 128:(tb + 1) * 128, :], reads=[("xs", tb)], writes=[("xp", tb % 2)])

                    def p1_front(tb):
                        pz = tb % 2
                        hf, hb, hT, qT, sm, sc = hfP[pz], hbP[pz], hTP[pz], qTP[pz], smP[pz], scP[pz]
                        tsl = slice(tb * 128, (tb + 1) * 128)
                        x_t = xt[tb % 2]; xk = ("xp", tb % 2)
                        tsl = slice(tb * 128, (tb + 1) * 128)
                        S.op("dve", lambda e: e.scalar_tensor_tensor(hf[:], x_t[:], 1.0, x_t[:], ALU.mult, ALU.mult, accum_out=sm[:, 0:1]),
                             reads=[xk], writes=[("hf", pz), ("sm0", pz)])
                        rstd_from_ssq(sm[:, 0:1], sm[:, 2:3], 1, D, ("sm0", pz), ("sm2", pz), ("sm1", pz), sm[:, 1:2])
                        S.op("dve", lambda e: e.scalar_tensor_tensor(hf[:], x_t[:], sm[:, 2:3], mod[:, 4, :], ALU.mult, ALU.mult),
                             reads=[xk, ("sm2", pz)], writes=[("hf", pz)])
                        S.op("dve", lambda e: e.tensor_tensor(hb[:], hf[:], mod[:, 3, :], ALU.add), reads=[("hf", pz)], writes=[("hb", pz)])
                        if tb + 2 < NB:
                            p1_load(tb + 2)
                        for dc in range(8):
                            S.op("pe", lambda e: e.transpose(psb(0)[:, dc * 128:(dc + 1) * 128], hb[:, dc * 128:(dc + 1) * 128], idb[:]),
                                 reads=[("hb", pz)], writes=[pb(0)])
                        S.op("act", lambda e: e.copy(hT[:].rearrange("p a b -> p (a b)"), psb(0)[:, :]), writes=[pb(0), ("hT", pz)])
                        S.dma("sp", h2s[tb], hT[:].rearrange("p a b -> p (a b)"), reads=[("hT", pz)], writes=[("h2s", tb)])
                        for hh in range(8):
                            bk = 1 + hh // 4
                            for dc in range(8):
                                S.op("pe", lambda e: e.matmul(ps[:, bk, (hh % 4) * 128:(hh % 4 + 1) * 128], wqb[:, dc, hh * 128:(hh + 1) * 128],
                                                              hT[:, dc, :], start=(dc == 0), stop=(dc == 7), skip_group_check=True),
                                     reads=[("hT", pz)], writes=[pb(bk)])
                        S.op("act", lambda e: e.copy(qT[:, 0:4, :].rearrange("p a b -> p (a b)"), ps[:, 1, :]), writes=[pb(1), ("qTa", pz)])
                        S.op("act", lambda e: e.copy(qT[:, 4:8, :].rearrange("p a b -> p (a b)"), ps[:, 2, :]), writes=[pb(2), ("qTb", pz)])
                        for hh in range(8):
                            bk = 3 + hh // 2
                            S.op("pe", lambda e: e.matmul(ps[:, bk, (hh % 2) * 256:(hh % 2 + 1) * 256], qT[:, hh, :], skb[:],
                                                          start=True, stop=True, skip_group_check=True),
                                 reads=[("qTa", pz), ("qTb", pz)], writes=[pb(bk)])
                        for b4 in range(4):
                            eng = "act"
                            if eng == "act":
                                S.op("act", lambda e: e.copy(sc[:, b4 * 512:(b4 + 1) * 512], ps[:, 3 + b4, :]), writes=[pb(3 + b4), ("sc", pz, b4)])
                            else:
                                S.op("dve", lambda e: e.tensor_copy(sc[:, b4 * 512:(b4 + 1) * 512], ps[:, 3 + b4, :]), writes=[pb(3 + b4), ("sc", pz, b4)])
                    def p1_back(tb):
                        pz = tb % 2
                        hf, hb, hT, qT, sm, sc = hfP[pz], hbP[pz], hTP[pz], qTP[pz], smP[pz], scP[pz]
                        tsl = slice(tb * 128, (tb + 1) * 128)
                        GS = [slice(gp * 128, (gp + 1) * 128) for gp in range(16)]
                        for gp in range(16):
                            S.op("dve", lambda e: e.max(sv[:, gp, 0:8], sc[:, GS[gp]]), reads=[("sc", pz, gp // 4)], writes=[("sv", gp)])
                        for gp in range(16):
                            S.op("dve", lambda e: e.max_index(si[:, gp, 0:8], sv[:, gp, 0:8], sc[:, GS[gp]]), reads=[("sc", pz, gp // 4), ("sv", gp)], writes=[("si", gp)])
                        for gp in range(16):
                            S.op("dve", lambda e: e.match_replace(sc2[:, GS[gp]], sv[:, gp, 0:8], sc[:, GS[gp]], -1e30), reads=[("sc", pz, gp // 4), ("sv", gp)], writes=[("sc2", gp)])
                        for gp in range(16):
                            S.op("dve", lambda e: e.max(sv[:, gp, 8:16], sc2[:, GS[gp]]), reads=[("sc2", gp)], writes=[("svb", gp)])
                        for gp in range(16):
                            S.op("dve", lambda e: e.max_index(si[:, gp, 8:16], sv[:, gp, 8:16], sc2[:, GS[gp]]), reads=[("sc2", gp), ("svb", gp)], writes=[("sib", gp)])
                        allsv = [("sv", g_) for g_ in range(16)] + [("svb", g_) for g_ in range(16)]
                        allsi = [("si", g_) for g_ in range(16)] + [("sib", g_) for g_ in range(16)]
                        S.op("dve", lambda e: e.tensor_copy(sif[:], si[:]), reads=allsi, writes=["sif"])
                        sv4 = sv[:].rearrange("p (h x) k -> p h x k", x=2)
                        S.op("dve", lambda e: e.tensor_tensor(cand[:].rearrange("p h (a b) -> p h a b", a=16),
                                                              sv4[:, :, 0, :].unsqueeze(3).to_broadcast([128, 8, 16, 16]),
                                                              sv4[:, :, 1, :].unsqueeze(2).to_broadcast([128, 8, 16, 16]), ALU.add),
                             reads=allsv, writes=["cand"])
                        for hh in range(8):
                            S.op("dve", lambda e: e.max(tsv[:, hh, 0:8], cand[:, hh, :]), reads=["cand"], writes=[("tsv", hh)])
                        for hh in range(8):
                            S.op("dve", lambda e: e.max_index(tiu[:, hh, 0:8], tsv[:, hh, 0:8], cand[:, hh, :]), reads=["cand", ("tsv", hh)], writes=[("tiu", hh)])
                        for hh in range(8):
                            S.op("dve", lambda e: e.match_replace(cand2[:, hh, :], tsv[:, hh, 0:8], cand[:, hh, :], -1e30), reads=["cand", ("tsv", hh)], writes=[("cand2", hh)])
                        for hh in range(8):
                            S.op("dve", lambda e: e.max(tsv[:, hh, 8:16], cand2[:, hh, :]), reads=[("cand2", hh)], writes=[("tsvb", hh)])
                        for hh in range(8):
                            S.op("dve", lambda e: e.max_index(tiu[:, hh, 8:16], tsv[:, hh, 8:16], cand2[:, hh, :]), reads=[("cand2", hh), ("tsvb", hh)], writes=[("tiub", hh)])
                        alltiu = [("tiu", h_) for h_ in range(8)] + [("tiub", h_) for h_ in range(8)]
                        alltsv = [("tsv", h_) for h_ in range(8)] + [("tsvb", h_) for h_ in range(8)]
                        tiuf = tiu[:].rearrange("p h k -> p (h k)")
                        S.op("dve", lambda e: e.tensor_single_scalar(au32[:], tiuf, c4u[:, 0:1], ALU.logical_shift_right), reads=alltiu, writes=["au32"])
                        S.op("dve", lambda e: e.tensor_single_scalar(bu32[:], tiuf, c15u[:, 0:1], ALU.bitwise_and), reads=alltiu, writes=["bu32"])
                        S.op("dve", lambda e: e.tensor_copy(af_[:], au32[:]), reads=["au32"], writes=["af_"])
                        S.op("dve", lambda e: e.tensor_copy(bf_[:], bu32[:]), reads=["bu32"], writes=["bf_"])
                        sif4 = sif[:].rearrange("p (h x) k -> p h x k", x=2)
                        for xx, sel in ((0, af_), (1, bf_)):
                            S.op("dve", lambda e: e.tensor_tensor(oh[:].rearrange("p (n a) -> p n a", a=16),
                                                                  iota16[:].unsqueeze(1).to_broadcast([128, 128, 16]),
                                                                  sel[:].unsqueeze(2).to_broadcast([128, 128, 16]), ALU.is_equal),
                                 reads=["af_", "bf_"], writes=["oh"])
                            S.op("dve", lambda e: e.tensor_tensor(oh[:].rearrange("p (h k a) -> p h k a", h=8, k=16),
                                                                  oh[:].rearrange("p (h k a) -> p h k a", h=8, k=16),
                                                                  sif4[:, :, xx, :].unsqueeze(2).to_broadcast([128, 8, 16, 16]), ALU.mult),
                                 reads=["sif"], writes=["oh"])
                            S.op("dve", lambda e: e.tensor_reduce(IJ[:, xx, :], oh[:].rearrange("p (n a) -> p n a", a=16), AX.X, ALU.add),
                                 reads=["oh"], writes=[("IJ", xx)])
                        S.op("dve", lambda e: e.tensor_tensor(ex[:], tsv[:], tsv[:, :, 0:1].to_broadcast([128, 8, 16]), ALU.subtract),
                             reads=alltsv, writes=["ex"])
                        S.op("act", lambda e: e.activation(ex[:], ex[:], AF.Exp), writes=["ex"])
                        S.op("dve", lambda e: e.tensor_reduce(zz[:, 0:8], ex[:], AX.X, ALU.add), reads=["ex"], writes=["zz0"])
                        S.op("dve", lambda e: e.reciprocal(zz[:, 8:16], zz[:, 0:8]), reads=["zz0"], writes=["zz1"])
                        S.op("dve", lambda e: e.tensor_tensor(gate[:].rearrange("p (h k) -> p h k", h=8), ex[:],
                                                              zz[:, 8:16].unsqueeze(2).to_broadcast([128, 8, 16]), ALU.mult),
                             reads=["ex", "zz1"], writes=["gate"])
                        S.op("pe", lambda e: e.transpose(ps[:, 7, 0:128], IJ[:, 0, :], idf[:]), reads=[("IJ", 0)], writes=[pb(7)])
                        S.op("pe", lambda e: e.transpose(ps[:, 7, 128:256], IJ[:, 1, :], idf[:]), reads=[("IJ", 1)], writes=[pb(7)])
                        S.op("pe", lambda e: e.transpose(ps[:, 7, 256:384], gate[:], idf[:]), reads=["gate"], writes=[pb(7)])
                        S.op("act", lambda e: e.copy(ijs[:], ps[:, 7, 0:384]), writes=[pb(7), "ijs"])
                        S.dma("sp", ijg.rearrange("k p t -> p k t")[:, :, tsl], ijs[:].rearrange("p (k t) -> p k t", k=3),
                              reads=["ijs"], writes=[("ijg", tb)])
                    p1_load(0)
                    p1_load(1)
                    p1_front(0)
                    for tb in range(NB):
                        if tb + 1 < NB:
                            p1_front(tb + 1)
                        p1_back(tb)
                    S.op("act", lambda e: e.copy(g2t[:], mod[:, 5, :]), writes=["g2t"])
                    S.barrier()
                if stop_after == ("P1", l):
                    S.barrier()
                    return nc

                PM.close()
                with ExitStack() as P:
                    def sb(name, shape, dt=F32):
                        return P.enter_context(nc.sbuf_tensor(f"{name}_{l}", list(shape), dt))
                    nsub = TB // 128
                    GTs = [sb(f"GT{i}", [128, TB, 128], BF16) for i in range(2)]
                    h2Ts = [sb(f"h2T{i}", [128, nsub, 8 * 128], BF16) for i in range(2)]
                    ijts = [sb(f"ijt{i}", [128, 3, TB]) for i in range(2)]
                    NOH = 4
                    OI = [sb(f"OI{i}", [128, 4, 128], BF16) for i in range(NOH)]
                    OJ = [sb(f"OJ{i}", [128, 4, 128], BF16) for i in range(NOH)]
                    NUB = 6
                    UB = [sb(f"UB{i}", [128, 8, 128], BF16) for i in range(NUB)]
                    VB = [sb(f"VB{i}", [128, D], BF16) for i in range(NUB)]
                    NAC = 3
                    AC = [sb(f"AC{i}", [128, TB], BF16) for i in range(NAC)]
                    WB = [sb(f"WB{i}", [128, TB], BF16) for i in range(NAC)]
                    xt = [sb("xq0", [128, D])] * 2
                    xo = [sb("xr0", [128, D])] * 2
                    yo = [sb("yo0", [128, D])] * 2
                    sm = sb("smq", [128, 8])
                    utb = utb_l[l]; vtb = vtb_l[l]
                    for tk in bg_toks:
                        S.wait_bg("sp", tk)
                    last = (l == n_layers - 1)
                    npass = NTOK // TB if p2_passes is None else p2_passes
                    NCH = p2_chunks
                    LA2 = 2
                    ijv = ijg.rearrange("k p t -> p k t")
                    gctr = [0]

                    def load_pass(pp):
                        z = pp % 2
                        for s_ in range(nsub):
                            tb = pp * nsub + s_
                            S.dma("sp", h2Ts[z][:, s_, :], h2s[tb], reads=[("h2s", tb)], writes=[("h2T", z, s_)])
                        S.dma("sp", ijts[z][:], ijv[:, :, pp * TB:(pp + 1) * TB],
                              reads=[("ijg", pp * nsub + s_) for s_ in range(nsub)], writes=[("ijt", z)])

                    def g_dve(pp, t4):
                        z = pp % 2
                        ijt = ijts[z]
                        sl = t4 % NOH
                        o_i = OI[sl]; o_j = OJ[sl]
                        t0 = t4 * 4
                        for u in range(4):
                            S.op("dve", lambda e: e.tensor_scalar(o_i[:, u, :], iota_b[:], ijt[:, 0, t0 + u:t0 + u + 1],
                                                                  ijt[:, 2, t0 + u:t0 + u + 1], ALU.is_equal, ALU.mult),
                                 reads=[("ijt", z)], writes=[("OI", sl, u)])
                        S.op("dve", lambda e: e.tensor_tensor(o_j[:], iota_b[:].unsqueeze(1).to_broadcast([128, 4, 128]),
                                                              ijt[:, 1, t0:t0 + 4].unsqueeze(2).to_broadcast([128, 4, 128]), ALU.is_equal),
                             reads=[("ijt", z)], writes=[("OJ", sl)])

                    def g_pe(pp, t4):
                        sl = t4 % NOH
                        o_i = OI[sl]; o_j = OJ[sl]
                        for u in range(4):
                            S.op("pe", lambda e: e.matmul(ps[:, 7, u * 128:(u + 1) * 128], o_j[:, u, :], o_i[:, u, :],
                                                          start=True, stop=True, skip_group_check=True),
                                 reads=[("OI", sl, u), ("OJ", sl)], writes=[pb(7)], inc=(u == 3))

                    def g_act(pp, t4):
                        z = pp % 2
                        t0 = t4 * 4
                        S.op("act", lambda e: e.copy(GTs[z][:, t0:t0 + 4, :].rearrange("p t i -> p (t i)"), ps[:, 7, :]),
                             writes=[pb(7), ("GT", z)])

                    def g_half(pp, t4):
                        g_dve(pp, t4); g_pe(pp, t4); g_act(pp, t4)

                    def emit_load(pp, i):
                        ub = UB[i % NUB]; vb = VB[i % NUB]
                        ubf = ub[:].rearrange("p a b -> p (a b)")
                        S.dma("sp", ubf, utb[i], writes=[("UB", i % NUB)])
                        S.dma("sp", vb[:], vtb[i], writes=[("VB", i % NUB)])

                    def emit_a(pp, i):
                        z = pp % 2
                        ub = UB[i % NUB]
                        ab = 4 + (i % 3)
                        for dc in range(8):
                            S.op("pe", lambda e: e.matmul(ps[:, ab, 0:TB].rearrange("p (s t) -> p s t", s=nsub), ub[:, dc, :],
                                                          h2Ts[z][:, :, dc * 128:(dc + 1) * 128],
                                                          start=(dc == 0), stop=(dc == 7)),
                                 reads=[("UB", i % NUB)] + [("h2T", z, s_) for s_ in range(nsub)], writes=[pb(ab)], inc=(dc == 7))
                        ac = AC[i % NAC]; wb = WB[i % NAC]
                        S.op("act", lambda e: e.activation(ac[:], ps[:, ab, 0:TB], AF.Gelu_apprx_tanh), writes=[pb(ab), ("AC", i % NAC)])
                        S.op("dve", lambda e: e.tensor_tensor(wb[:], ac[:], GTs[z][:, :, i], ALU.mult),
                             reads=[("AC", i % NAC), ("GT", z)], writes=[("WB", i % NAC)])

                    def emit_v(pp, i):
                        vb = VB[i % NUB]; wb = WB[i % NAC]
                        for s_ in range(nsub):
                            for dh in range(2):
                                S.op("pe", lambda e: e.matmul(ps[:, 2 * s_ + dh, :], wb[:, s_ * 128:(s_ + 1) * 128], vb[:, dh * 512:(dh + 1) * 512],
                                                              start=(i == 0), stop=(i == NCH - 1)),
                                     reads=[("WB", i % NAC), ("VB", i % NUB)], writes=[pb(2 * s_ + dh)], inc=(s_ == nsub - 1 and dh == 1))

                    load_pass(0)
                    for t4 in range(TB // 4):
                        g_half(0, t4)
                    PF = NUB - 1
                    for pp in range(npass):
                        if pp + 1 < npass:
                            load_pass(pp + 1)
                        for i in range(min(PF, NCH)):
                            emit_load(pp, i)
                        NH = TB // 4
                        for n in range(NCH + LA2 + 4):
                            if n < NCH:
                                emit_a(pp, n)
                            if 0 <= n - LA2 < NCH:
                                emit_v(pp, n - LA2)
                                if n - LA2 + PF < NCH:
                                    emit_load(pp, n - LA2 + PF)
                            if pp + 1 < npass:
                                if n % 2 == 0 and n // 2 < NH:
                                    g_dve(pp + 1, n // 2)
                                if n >= 2 and n % 2 == 0 and (n - 2) // 2 < NH:
                                    g_pe(pp + 1, (n - 2) // 2)
                                if n >= 3 and n % 2 == 1 and (n - 3) // 2 < NH:
                                    g_act(pp + 1, (n - 3) // 2)
                        for s_ in range(nsub):
                            tb = pp * nsub + s_
                            tsl = slice(tb * 128, (tb + 1) * 128)
                            x_t = xt[0]; xk = ("xq", 0)
                            x_o = xo[0]; ok2 = ("xr", 0)
                            S.dma("sp", x_t[:], xs[tsl, :], reads=[("xs", tb)], writes=[xk])
                            for dh in range(2):
                                csl = slice(dh * 512, (dh + 1) * 512)
                                S.op("dve", lambda e: e.tensor_tensor(x_o[:, csl], ps[:, 2 * s_ + dh, :], g2t[:, csl], ALU.mult),
                                     writes=[pb(2 * s_ + dh), ok2])
                            S.op("pool", lambda e: e.tensor_tensor(x_o[:], x_o[:], x_t[:], ALU.add), reads=[xk], writes=[ok2])
                            if not last:
                                S.dma("sp", xs[tsl, :], x_o[:], reads=[ok2], writes=[("xs", tb)])
                            else:
                                y_o = yo[0]; yk = ("yo", 0)
                                S.op("dve", lambda e: e.scalar_tensor_tensor(y_o[:], x_o[:], 1.0, x_o[:], ALU.mult, ALU.mult, accum_out=sm[:, 0:1]),
                                     reads=[ok2], writes=[yk, "smq0"])
                                rstd_from_ssq(sm[:, 0:1], sm[:, 2:3], 1, D, "smq0", "smq2", "smq1", sm[:, 1:2])
                                S.op("dve", lambda e: e.scalar_tensor_tensor(y_o[:], x_o[:], sm[:, 2:3], fgb[:], ALU.mult, ALU.mult),
                                     reads=[ok2, "smq2"], writes=[yk])
                                S.dma("sp", y_out[tsl, :], y_o[:], reads=[yk], writes=[("y", tb)])
                    S.barrier()
        S.barrier()
    return nc


def _rope_tables(n_tokens, dim):
    rows = n_tokens // 64
    row = np.repeat(np.arange(rows, dtype=np.float32), 64)
    col = np.tile(np.arange(64, dtype=np.float32), rows)
    quarter = dim // 4
    inv = (np.float32(10000.0) ** (-np.arange(quarter, dtype=np.float32) / np.float32(quarter))).astype(np.float32)
    ang = np.concatenate([row[:, None] * inv, col[:, None] * inv], axis=-1).astype(np.float32)
    return np.cos(ang).astype(np.float32), np.sin(ang).astype(np.float32)


_NC_CACHE = {}


def prepare_inputs(x_prompt, x_sample, cache_diff_k, cache_diff_v, cache_gqa_k, cache_gqa_v, c, c_ctx,
                   ada_w, ada_b, norm1_g, norm2_g, w_in, sgu_norm_g, sgu_w, sgu_b,
                   diff_lq1, diff_lk1, diff_lq2, diff_lk2, diff_subln_g, gqa_qnorm_g, gqa_knorm_g, w_out,
                   peer_wq, peer_subkeys, peer_u, peer_v, final_g, cores=range(8)):
    f = lambda a: np.ascontiguousarray(np.asarray(a, dtype=np.float32))
    x_prompt = f(x_prompt); x_sample = f(x_sample)
    peer_u = f(peer_u); peer_v = f(peer_v)
    utl = np.ascontiguousarray(peer_u.reshape(L, 128, 128, 8, 128).transpose(0, 1, 4, 3, 2))
    vtl = peer_v.reshape(L, 128 * 128, D)
    sk = f(peer_subkeys)
    skbd = np.zeros((L, 128, 256), np.float32)
    for x in range(2):
        skbd[:, x * 64:(x + 1) * 64, x * 128:(x + 1) * 128] = sk[:, x].transpose(0, 2, 1)
    shared = {
        "ada_w": f(ada_w), "ada_b": f(ada_b), "norm1_g": f(norm1_g), "norm2_g": f(norm2_g),
        "w_in": f(w_in), "sgu_norm_g": f(sgu_norm_g),
        "sgu_wT": np.ascontiguousarray(f(sgu_w).transpose(0, 1, 3, 2)),
        "sgu_bT": np.ascontiguousarray(f(sgu_b).transpose(0, 2, 1)),
        "lq1": f(diff_lq1), "lk1": f(diff_lk1), "lq2": f(diff_lq2), "lk2": f(diff_lk2),
        "subln_g": f(diff_subln_g), "qn_g": f(gqa_qnorm_g), "kn_g": f(gqa_knorm_g),
        "w_out": f(w_out), "wq": f(peer_wq), "skbd": skbd, "ut": utl, "vt": vtl,
        "final_g": f(final_g).reshape(1, D),
    }
    cd, sd = _rope_tables(NTOK, 32)
    cg, sg = _rope_tables(NTOK, 64)
    mp = np.zeros((NKB, 8), np.float32)
    for kb in range(NKB):
        for q in range(8):
            if kb >= NB or kb // 2 != q:
                mp[kb, q] = NEG
    mp = np.broadcast_to(mp.reshape(1, -1), (128, NKB * 8)).copy()
    ms = np.zeros((128, NKB * 8), np.float32)
    cdk = f(cache_diff_k).reshape(4, L, PAST, 256); cdv = f(cache_diff_v).reshape(4, L, PAST, 256)
    cgk = f(cache_gqa_k).reshape(4, L, PAST, 128); cgv = f(cache_gqa_v).reshape(4, L, PAST, 128)
    z256 = np.zeros((L, PAST, 256), np.float32); z128 = np.zeros((L, PAST, 128), np.float32)
    ones16 = np.ones((NTOK, 16), np.float32); zer16 = np.zeros((NTOK, 16), np.float32)
    ones32 = np.ones((NTOK, 32), np.float32); zer32 = np.zeros((NTOK, 32), np.float32)
    c = f(c); c_ctx = f(c_ctx)
    in_maps = []
    for core in cores:
        m = dict(shared)
        if core < 4:
            m["x"] = x_prompt[8 * core:8 * core + 8].reshape(NTOK, D)
            cv = c_ctx
            m.update(cdk=z256, cdv=z256, cgk=z128, cgv=z128, maskb=mp,
                     cosd=ones16, sind=zer16, cosg=ones32, sing=zer32)
        else:
            b = core - 4
            m["x"] = x_sample[b]
            cv = c[b]
            m.update(cdk=cdk[b], cdv=cdv[b], cgk=cgk[b], cgv=cgv[b], maskb=ms,
                     cosd=cd, sind=sd, cosg=cg, sing=sg)
        m["cvec"] = np.ascontiguousarray(cv.reshape(8, 128).T)
        in_maps.append(m)
    return in_maps


def kernel(**inputs):
    in_maps = prepare_inputs(**inputs)
    if "nc" not in _NC_CACHE:
        _NC_CACHE["nc"] = build_program()
    nc = _NC_CACHE["nc"]
    res = run_bass_kernel_spmd(nc, in_maps, core_ids=list(range(8)))
    r = res.results
    y_prompt = np.concatenate([r[i]["y"].reshape(8, 256, D) for i in range(4)], axis=0)
    y_sample = np.stack([r[4 + i]["y"] for i in range(4)], axis=0)

    def gather(name, tail):
        parts = []
        for i in range(4):
            a = r[i][name]
            a = a.reshape(L, 8, 256, -1).transpose(1, 0, 2, 3)
            parts.append(a)
        return np.ascontiguousarray(np.concatenate(parts, axis=0).reshape((32, L, 256) + tail))

    return (np.ascontiguousarray(y_prompt.astype(np.float32)), np.ascontiguousarray(y_sample.astype(np.float32)),
            gather("ndk", (4, 2, 32)), gather("ndv", (4, 64)), gather("ngk", (2, 64)), gather("ngv", (2, 64)))
```
